# Optimizing a Trainium2 kernel written in Bass

```python
import jax, jax.numpy as jnp
from jax import lax
import numpy as np

D_MODEL = 2048
BATCH = 2
SEQ = 4096
DEPTH = 2
DEC_BATCH = 128
DEC_SEQ = 4
PAST_LEN = 8192
PAGE_SIZE = 128

MIX_DIM = D_MODEL
POOL_DIM = MIX_DIM // 2
POOL_WINDOWS = (2, 4, 8, 16)
N_POOL_GROUPS = len(POOL_WINDOWS)
POOL_GROUP_DIM = POOL_DIM // N_POOL_GROUPS
POOL_HIST = max(POOL_WINDOWS) - 1
HEAD_DIM = 64
N_HEADS = (MIX_DIM - POOL_DIM) // HEAD_DIM
N_KV_HEADS = 4
GROUP = N_HEADS // N_KV_HEADS
Q_DIM = N_HEADS * HEAD_DIM
KV_DIM = N_KV_HEADS * HEAD_DIM
IN_DIM = POOL_DIM + Q_DIM + 2 * KV_DIM
WINDOW = 128
BLOCK = WINDOW
FFN_DIM = 5632
RMS_EPS = 1e-5
ATTN_SCALE = HEAD_DIM ** -0.5

kernel_name = 'hymba_pool_swa_sink_macaron_step'


def _rmsnorm(x, g):
    xf = x.astype(jnp.float32)
    y = xf * lax.rsqrt(jnp.mean(xf * xf, axis=-1, keepdims=True) + RMS_EPS)
    return (y * g.astype(jnp.float32)).astype(x.dtype)


def _swiglu(x, wg, wu, wd):
    return (jax.nn.silu(x @ wg) * (x @ wu)) @ wd


def _pool_mix(u, hist, n_hist, pool_w, pool_scale):
    B, T, _ = u.shape
    h = jnp.concatenate([hist, u], axis=1)
    hf = h.astype(jnp.float32)
    cs = jnp.concatenate([jnp.zeros((B, 1, POOL_DIM), jnp.float32), jnp.cumsum(hf, axis=1)], axis=1)
    pos = jnp.arange(T)
    start = POOL_HIST + 1
    groups = []
    for g, w in enumerate(POOL_WINDOWS):
        sl = slice(g * POOL_GROUP_DIM, (g + 1) * POOL_GROUP_DIM)
        s = cs[:, start:start + T, sl] - cs[:, start - w:start - w + T, sl]
        cnt = jnp.minimum(w, n_hist + pos + 1).astype(jnp.float32)
        groups.append(s / cnt[None, :, None] - hf[:, POOL_HIST:, sl])
    d = jnp.stack(groups, axis=2).astype(u.dtype)
    y = jnp.einsum('btgc,gcd->btgd', d, pool_w).reshape(B, T, POOL_DIM) * pool_scale
    return y, h[:, -POOL_HIST:]


def _sink_softmax(scores, valid, sink):
    scores = jnp.where(valid, scores, -jnp.inf)
    m = jnp.maximum(jnp.max(scores, axis=-1, keepdims=True), sink)
    p = jnp.exp(scores - m)
    return p / (jnp.sum(p, axis=-1, keepdims=True) + jnp.exp(sink - m))


def _swa_prompt(q, k, v, sinks):
    B, S, _ = q.shape
    nb = S // BLOCK
    qb = q.reshape(B, nb, BLOCK, N_KV_HEADS, GROUP, HEAD_DIM)
    kb = k.reshape(B, nb, BLOCK, N_KV_HEADS, HEAD_DIM)
    vb = v.reshape(B, nb, BLOCK, N_KV_HEADS, HEAD_DIM)

    def with_prev(t):
        prev = jnp.concatenate([jnp.zeros_like(t[:, :1]), t[:, :-1]], axis=1)
        return jnp.concatenate([prev, t], axis=2)

    kk, vv = with_prev(kb), with_prev(vb)
    scores = jnp.einsum('bnqkgd,bnskd->bnkgqs', qb, kk,
                        preferred_element_type=jnp.float32) * ATTN_SCALE
    i = jnp.arange(BLOCK)[:, None]
    j = jnp.arange(2 * BLOCK)[None, :]
    rel = i + BLOCK - j
    key_pos = (jnp.arange(nb)[:, None, None] - 1) * BLOCK + j[None]
    valid = (rel >= 0)[None] & (rel < WINDOW)[None] & (key_pos >= 0)
    valid = valid[None, :, None, None]
    sink = sinks.astype(jnp.float32).reshape(1, 1, N_KV_HEADS, GROUP, 1, 1)
    p = _sink_softmax(scores, valid, sink)
    out = jnp.einsum('bnkgqs,bnskd->bnqkgd', p.astype(v.dtype), vv)
    return out.reshape(B, S, Q_DIM)


def _swa_sample(q, k_all, v_all, sinks, n_buf):
    B, T, _ = q.shape
    qg = q.reshape(B, T, N_KV_HEADS, GROUP, HEAD_DIM)
    scores = jnp.einsum('btkgd,bskd->bkgts', qg, k_all,
                        preferred_element_type=jnp.float32) * ATTN_SCALE
    rel = n_buf + jnp.arange(T)[:, None] - jnp.arange(n_buf + T)[None, :]
    valid = (rel >= 0) & (rel < WINDOW)
    sink = sinks.astype(jnp.float32).reshape(1, N_KV_HEADS, GROUP, 1, 1)
    p = _sink_softmax(scores, valid, sink)
    out = jnp.einsum('bkgts,bskd->btkgd', p.astype(v_all.dtype), v_all)
    return out.reshape(B, T, Q_DIM)


def _layer(x, pool_hist, n_hist, k_hist, v_hist,
           n1, wg1, wu1, wd1, nm, w_in, pool_w, pool_scale, sinks, w_out, n2, wg2, wu2, wd2):
    B, T, _ = x.shape
    x = x + 0.5 * _swiglu(_rmsnorm(x, n1), wg1, wu1, wd1)
    proj = _rmsnorm(x, nm) @ w_in
    u = proj[..., :POOL_DIM]
    q = proj[..., POOL_DIM:POOL_DIM + Q_DIM]
    k = proj[..., POOL_DIM + Q_DIM:POOL_DIM + Q_DIM + KV_DIM].reshape(B, T, N_KV_HEADS, HEAD_DIM)
    v = proj[..., POOL_DIM + Q_DIM + KV_DIM:].reshape(B, T, N_KV_HEADS, HEAD_DIM)
    pool_out, new_pool = _pool_mix(u, pool_hist, n_hist, pool_w, pool_scale)
    if k_hist is None:
        attn = _swa_prompt(q, k, v, sinks)
        keep = min(WINDOW, T)
        new_k, new_v = k[:, T - keep:], v[:, T - keep:]
    else:
        n_buf = k_hist.shape[1]
        k_all = jnp.concatenate([k_hist, k], axis=1)
        v_all = jnp.concatenate([v_hist, v], axis=1)
        attn = _swa_sample(q, k_all, v_all, sinks, n_buf)
        new_k, new_v = k_all[:, T:], v_all[:, T:]
    x = x + jnp.concatenate([pool_out, attn], axis=-1) @ w_out
    x = x + 0.5 * _swiglu(_rmsnorm(x, n2), wg2, wu2, wd2)
    return x, new_k, new_v, new_pool


def setup_inputs(seed: int = 0) -> dict:
    key = jax.random.key(seed)
    ks = jax.random.split(key, 24)
    f32 = jnp.float32
    n_buf = min(WINDOW, PAST_LEN)

    def nrm(k, shape, scale):
        return jax.random.normal(k, shape, f32) * scale

    def gain(k, shape):
        return 1.0 + 0.02 * jax.random.normal(k, shape, f32)

    return {
        'x_prompt': nrm(ks[0], (BATCH, SEQ, D_MODEL), 1.0),
        'x_sample': nrm(ks[1], (DEC_BATCH, DEC_SEQ, D_MODEL), 1.0),
        'cache_k': nrm(ks[2], (DEPTH, DEC_BATCH, n_buf, N_KV_HEADS, HEAD_DIM), 1.0),
        'cache_v': nrm(ks[3], (DEPTH, DEC_BATCH, n_buf, N_KV_HEADS, HEAD_DIM), 1.0),
        'state_pool': nrm(ks[4], (DEPTH, DEC_BATCH, POOL_HIST, POOL_DIM), 1.0),
        'norm_ffn1': gain(ks[5], (DEPTH, D_MODEL)),
        'ffn1_gate': nrm(ks[6], (DEPTH, D_MODEL, FFN_DIM), D_MODEL ** -0.5),
        'ffn1_up': nrm(ks[7], (DEPTH, D_MODEL, FFN_DIM), D_MODEL ** -0.5),
        'ffn1_down': nrm(ks[8], (DEPTH, FFN_DIM, D_MODEL), FFN_DIM ** -0.5),
        'norm_mix': gain(ks[9], (DEPTH, D_MODEL)),
        'w_in': nrm(ks[10], (DEPTH, D_MODEL, IN_DIM), D_MODEL ** -0.5),
        'pool_w': nrm(ks[11], (DEPTH, N_POOL_GROUPS, POOL_GROUP_DIM, POOL_GROUP_DIM), POOL_GROUP_DIM ** -0.5),
        'pool_scale': gain(ks[12], (DEPTH, POOL_DIM)),
        'attn_sinks': nrm(ks[13], (DEPTH, N_HEADS), 0.5),
        'w_out': nrm(ks[14], (DEPTH, MIX_DIM, D_MODEL), MIX_DIM ** -0.5),
        'norm_ffn2': gain(ks[15], (DEPTH, D_MODEL)),
        'ffn2_gate': nrm(ks[16], (DEPTH, D_MODEL, FFN_DIM), D_MODEL ** -0.5),
        'ffn2_up': nrm(ks[17], (DEPTH, D_MODEL, FFN_DIM), D_MODEL ** -0.5),
        'ffn2_down': nrm(ks[18], (DEPTH, FFN_DIM, D_MODEL), FFN_DIM ** -0.5),
        'final_norm': gain(ks[19], (D_MODEL,)),
    }


def reference(x_prompt, x_sample, cache_k, cache_v, state_pool,
              norm_ffn1, ffn1_gate, ffn1_up, ffn1_down, norm_mix, w_in, pool_w, pool_scale,
              attn_sinks, w_out, norm_ffn2, ffn2_gate, ffn2_up, ffn2_down, final_norm):
    yp, ys = x_prompt, x_sample
    n_hist_sample = min(POOL_HIST, PAST_LEN)
    kp_l, vp_l, pp_l, ks_l, vs_l, ps_l = [], [], [], [], [], []
    for l in range(DEPTH):
        w = (norm_ffn1[l], ffn1_gate[l], ffn1_up[l], ffn1_down[l], norm_mix[l], w_in[l],
             pool_w[l], pool_scale[l], attn_sinks[l], w_out[l], norm_ffn2[l],
             ffn2_gate[l], ffn2_up[l], ffn2_down[l])
        zero_hist = jnp.zeros((yp.shape[0], POOL_HIST, POOL_DIM), yp.dtype)
        yp, kp, vp, pp = _layer(yp, zero_hist, 0, None, None, *w)
        ys, ks_, vs_, ps_ = _layer(ys, state_pool[l], n_hist_sample, cache_k[l], cache_v[l], *w)
        kp_l.append(kp); vp_l.append(vp); pp_l.append(pp)
        ks_l.append(ks_); vs_l.append(vs_); ps_l.append(ps_)
    y_prompt = _rmsnorm(yp, final_norm)
    y_sample = _rmsnorm(ys, final_norm)
    return (y_prompt, y_sample,
            jnp.stack(kp_l), jnp.stack(vp_l), jnp.stack(pp_l),
            jnp.stack(ks_l), jnp.stack(vs_l), jnp.stack(ps_l))
```

```python
import numpy as np
from contextlib import ExitStack
import concourse.bass as bass
import concourse.mybir as mybir
from concourse.bass_utils import run_bass_kernel_spmd

F32 = mybir.dt.float32
BF16 = mybir.dt.bfloat16
AF = mybir.ActivationFunctionType
ALU = mybir.AluOpType
AX = mybir.AxisListType

D = 2048; NCH = 16; FF = 5632; NF = 44; NT = 1344
TILES = [(0, 448), (448, 896), (896, 1344)]
MTILES = [(0, 384), (384, 768), (768, 1152), (1152, 1344)]


def tiles_from(c0):
    n = NT - c0
    a = ((n + 2) // 3 + 1) // 2 * 2
    return [(c0, c0 + a), (c0 + a, c0 + 2 * a), (c0 + 2 * a, NT)]
SC = 1280
SCALE = 0.125
EPS = 1e-5
POOLW = (2, 4, 8, 16)
import os
STOP_AFTER = int(os.environ.get('KSTOP', '6'))
KSMALL = int(os.environ.get('KSMALL', '0'))
KFLAGS = os.environ.get('KFLAGS', '')
MSTOP = int(os.environ.get('MSTOP', '5'))


def units(a, b):
    return range(a // 64, (b + 63) // 64)


def K(name, cs, a, b):
    return [(name, c, u) for c in cs for u in units(a, b)]


class Prog:
    def __init__(s, nc, es):
        s.nc = nc; s.es = es; s.ops = []
        s.E = {'pe': nc.tensor, 'act': nc.scalar, 'dve': nc.vector, 'pool': nc.gpsimd, 'sp': nc.sync}
        s.out_sems = set()

    def op(s, eng, fn, R=(), W=()):
        s.ops.append(('c', eng, fn, list(R), list(W), None))

    def dma(s, q, out, in_, sem, R=(), W=(), is_out=False):
        s.ops.append(('d', q, (out, in_), list(R), list(W), sem))
        if is_out:
            s.out_sems.add(sem)

    def emit(s):
        nc = s.nc; ops = s.ops; n = len(ops)
        lastw = {}; readers = {}
        deps = [None] * n
        for i, (kind, eng, fn, R, W, sem) in enumerate(ops):
            d = set()
            for k in R:
                j = lastw.get(k)
                if j is not None: d.add(j)
            for k in W:
                j = lastw.get(k)
                if j is not None: d.add(j)
                rd = readers.get(k)
                if rd:
                    d.update(rd.values())
            d.discard(i)
            deps[i] = d
            rk = eng if kind == 'c' else ('dma', i)
            for k in W:
                lastw[k] = i; readers[k] = {}
            for k in R:
                readers.setdefault(k, {})[rk] = i
        seqno = [0] * n; ectr = {}
        for i in range(n):
            if ops[i][0] == 'c':
                ectr[ops[i][1]] = ectr.get(ops[i][1], 0) + 1
                seqno[i] = ectr[ops[i][1]]
        NEAR = 6

        def same_skip(i, j):
            if not (ops[i][0] == 'c' and ops[j][0] == 'c' and ops[i][1] == ops[j][1]):
                return False
            return ops[i][1] == 'pe' or (seqno[i] - seqno[j]) > NEAR
        needed = [False] * n
        for i in range(n):
            for j in deps[i]:
                if ops[j][0] == 'c' and not same_skip(i, j):
                    needed[j] = True
        cnt = {}; idx = [0] * n; dcount_before = [None] * n
        dsem_cnt = {}
        for i, (kind, eng, fn, R, W, sem) in enumerate(ops):
            if kind == 'c':
                if needed[i]:
                    cnt[eng] = cnt.get(eng, 0) + 1
                    idx[i] = cnt[eng]
            else:
                dsem_cnt[sem] = dsem_cnt.get(sem, 0) + 16
                idx[i] = dsem_cnt[sem]
        csem = {e: s.es.enter_context(nc.semaphore('c_' + e)) for e in ['pe', 'act', 'dve', 'pool']}
        dsem = {k: s.es.enter_context(nc.semaphore('d_' + str(k))) for k in dsem_cnt}
        seen = {e: {} for e in s.E}
        run_d = {k: 0 for k in dsem_cnt}
        nwaits = 0
        for i, (kind, eng, fn, R, W, sem) in enumerate(ops):
            Eng = s.E[eng]
            waits = {}
            for j in deps[i]:
                kj, ej = ops[j][0], ops[j][1]
                if kj == 'd':
                    key = ('d', ops[j][5]); val = run_d[ops[j][5]]
                elif same_skip(i, j):
                    continue
                else:
                    key = ('c', ej); val = idx[j]
                if val > waits.get(key, 0):
                    waits[key] = val
            for key, val in waits.items():
                if seen[eng].get(key, 0) >= val:
                    continue
                seen[eng][key] = val
                Eng.wait_ge(dsem[key[1]] if key[0] == 'd' else csem[key[1]], val)
                nwaits += 1
            if kind == 'c':
                ins = fn(Eng)
                if needed[i]:
                    ins.then_inc(csem[eng], 1)
            else:
                out, in_ = fn
                with nc.allow_non_contiguous_dma(reason="layout"):
                    Eng.dma_start(out=out, in_=in_).then_inc(dsem[sem], 16)
                run_d[sem] += 16
        for sem in sorted(s.out_sems, key=str):
            nc.sync.wait_ge(dsem[sem], dsem_cnt[sem])
        return n, nwaits


def build():
    nc = bass.Bass("TRN2", target_bir_lowering=False)
    es = ExitStack()
    P = Prog(nc, es)

    def din(name, shape):
        return nc.dram_tensor(name, list(shape), F32, kind="ExternalInput").ap()

    def dout(name, shape):
        return nc.dram_tensor(name, list(shape), F32, kind="ExternalOutput").ap()

    xin = din("xin", [NT, D])
    if KSMALL:
        Wg = Wu = Wd = [None, None]; w_in = w_out = None
        if KSMALL == 2:
            w_in = din("w_in", [2, D, 2560]); w_out = din("w_out", [2, D, D])
    else:
        Wg = [din("ffn1_gate", [2, D, FF]), din("ffn2_gate", [2, D, FF])]
        Wu = [din("ffn1_up", [2, D, FF]), din("ffn2_up", [2, D, FF])]
        Wd = [din("ffn1_down", [2, FF, D]), din("ffn2_down", [2, FF, D])]
        w_in = din("w_in", [2, D, 2560]); w_out = din("w_out", [2, D, D])
    ck = din("ck", [2, 16, 128, 256]); cv = din("cv", [2, 16, 128, 256]); spool = din("spool", [2, 16, 15, 1024])
    norms = din("norms", [7, D])
    pool_w = din("pool_w", [2, 4, 256, 256]); pscale_d = din("pool_scale", [2, 1024]); sinks_d = din("attn_sinks", [2, 16])
    identf_d = din("identf", [128, 128]); masks_d = din("masks", [128, 512]); smask_d = din("smask", [128, 1152])
    sel_d = din("sel", [128, 128]); invc_d = din("invc", [128, 64])
    y = dout("y", [1088, D]); nkp = dout("nkp", [2, 128, 256]); nvp = dout("nvp", [2, 128, 256]); npp = dout("npp", [2, 15, 1024])
    nks = dout("nks", [2, 16, 128, 256]); nvs = dout("nvs", [2, 16, 128, 256]); nps = dout("nps", [2, 16, 15, 1024])

    def sb(name, shape, dt):
        return es.enter_context(nc.sbuf_tensor(name, list(shape), dt))

    R1 = sb("R1", [128, NCH * NT], F32)
    R2 = sb("R2", [128, 21504], BF16)
    R3 = sb("R3", [128, 21504], BF16)
    R4 = sb("R4", [128, 8192], BF16)
    identf = sb("identf_s", [128, 128], F32); identb = sb("identb", [128, 128], BF16); onesb = sb("onesb", [128, 128], BF16); nhl = sb("nhl", [128, 896], BF16)
    gam = sb("gam", [128, 7 * 16], F32); pscale = sb("pscale", [128, 16], F32)
    nsink = sb("nsink", [128, 32], F32); nsrow = sb("nsrow", [128, 8], F32)
    masks = sb("masks_s", [128, 512], BF16); smask = sb("smask_s", [128, 1152], BF16); sel = sb("sel_s", [128, 128], BF16)
    invc = sb("invc_s", [128, 64], F32); epsb = sb("epsb", [128, 1], F32)
    ntmp = sb("ntmp", [128, 5 * 448], F32)
    stmp = sb("stmp", [128, 2 * 448], F32)
    PSA = es.enter_context(nc.psum_tensor("PSA", [128, 2048], F32))
    PSB = es.enter_context(nc.psum_tensor("PSB", [128, 2048], F32))

    def bank(b, n=512, off=0):
        t = PSA if b < 4 else PSB
        return t[:, (b % 4) * 512 + off:(b % 4) * 512 + off + n]

    def bankb(b):
        return bank(b).bitcast(BF16)

    x3 = R1[:, :].rearrange("p (c n) -> p c n", c=NCH)

    def r2(off, nbytes, dt=BF16):
        v = R2[:, off // 2:(off + nbytes) // 2]
        return v.bitcast(F32) if dt == F32 else v

    def r3(off, nbytes, dt=BF16):
        v = R3[:, off // 2:(off + nbytes) // 2]
        return v.bitcast(F32) if dt == F32 else v

    xn3 = R2[:, :].rearrange("p (c n) -> p c n", c=NCH)
    mix3 = R3[:, :].rearrange("p (c n) -> p c n", c=NCH)
    Hb = [r3(i * 10752, 10752).rearrange("p (f n) -> p f n", f=4) for i in range(2)]
    Wdb = [r3(21504 + i * 8192, 8192).rearrange("p (f n) -> p f n", f=4) for i in range(2)]
    stage = [r3(i * 8192, 8192, F32) for i in range(2)]
    wslot = [R4[:, i * 2048:(i + 1) * 2048].rearrange("p (k n) -> p k n", k=16) for i in range(4)]
    xnt3 = r2(0, 14336).rearrange("p (c n) -> p c n", c=NCH)
    kT3 = r2(14336, 10752).rearrange("p (g n) -> p g n", g=4)
    vtok = r2(25088, 5632).rearrange("p (b n) -> p b n", b=11)
    TM = 30720

    nsq = [ntmp[:, i * 448:(i + 1) * 448] for i in range(2)]
    nacc = ntmp[:, 896:1344]; nrs = ntmp[:, 1344:1792]; nrstd = ntmp[:, 1792:2240]
    sil = [stmp[:, i * 448:(i + 1) * 448] for i in range(2)]

    wctr = [0]

    def next_wslot():
        i = wctr[0] % 4; wctr[0] += 1
        return i

    P.dma('sp', identf[:, :], identf_d[:, :], 'cst', W=['identf'])
    P.dma('pool', masks[:, :], masks_d[:, :], 'cstp', W=['masks'])
    P.dma('pool', smask[:, :], smask_d[:, :], 'cstp', W=['smask'])
    P.dma('pool', sel[:, :], sel_d[:, :], 'cstp', W=['sel'])
    P.dma('pool', identb[:, :], identf_d[:, :], 'cstp', W=['identb'])
    P.dma('sp', invc[:, :], invc_d[:, :], 'cst', W=['invc'])
    P.dma('sp', stage[0][0:112, 0:128], norms.rearrange("j (c p) -> (j c) p", p=128), 'cst', W=[('stage', 0)])
    P.dma('sp', stage[0][0:16, 128:256], pscale_d.rearrange("l (c p) -> (l c) p", p=128), 'cst', W=[('stage', 0)])
    if 'nosink' not in KFLAGS:
        P.dma('sp', nsink[:, :], sinks_d.rearrange("l h -> (l h)").partition_broadcast(128), 'cst', W=['nsink'])
    sflat = sinks_d.rearrange("l h -> (l h)")
    for hp in range(2):
        for hh in range(2):
            src = sflat.rearrange("(q h) -> h q", h=4)[2 * hh + hp].partition_broadcast(32)
            if 'nosink' not in KFLAGS:
                P.dma('sp', nsrow[hp * 64 + hh * 32:hp * 64 + hh * 32 + 32, :], src, 'cst', W=['nsrow'])
    dummy = sb("dummy_bar", [128, 8], F32)

    def barrier(R, W):
        P.op('dve', lambda e: e.memset(dummy[:, 0:1], 0.0), R=R, W=W)

    ALLXN = K('xn', range(NCH), 0, NT)
    ALLMIX = K('mix', range(NCH), 0, NT)
    P.op('dve', lambda e: e.memset(onesb[:, :], 1.0), W=['onesb'])
    P.op('pe', lambda e: e.transpose(bank(0)[:, 0:112], stage[0][0:112, 0:128], identf[0:112, 0:112]), R=[('stage', 0), 'identf'], W=[('ps', 0)])
    P.op('pe', lambda e: e.transpose(bank(0)[:, 128:144], stage[0][0:16, 128:256], identf[0:16, 0:16]), R=[('stage', 0), 'identf'], W=[('ps', 0)])
    P.op('dve', lambda e: e.tensor_copy(out=gam[:, :], in_=bank(0)[:, 0:112]), R=[('ps', 0)], W=['gam'])
    P.op('dve', lambda e: e.tensor_copy(out=pscale[:, :], in_=bank(0)[:, 128:144]), R=[('ps', 0)], W=['pscale'])
    P.op('dve', lambda e: e.memset(epsb[:, :], EPS), W=['epsb'])
    P.op('dve', lambda e: e.tensor_scalar(out=nsink[:, :], in0=nsink[:, :], scalar1=-1.0, scalar2=None, op0=ALU.mult), R=['nsink'], W=['nsink'])
    P.op('dve', lambda e: e.tensor_scalar(out=nsrow[:, :], in0=nsrow[:, :], scalar1=-1.0, scalar2=None, op0=ALU.mult), R=['nsrow'], W=['nsrow'])

    evq = [0]

    def evac(out, in_, R, W):
        evq[0] += 1
        if evq[0] % 2:
            P.op('act', lambda e: e.activation(out=out, in_=in_, func=AF.Copy), R=R, W=W)
        else:
            P.op('dve', lambda e: e.tensor_copy(out=out, in_=in_), R=R, W=W)

    pb = 0
    for tb in range(11):
        rows = 128 if tb < 10 else 64
        st = stage[tb % 2]
        P.dma('sp', st[0:rows, :], xin[tb * 128:tb * 128 + rows, :], 'stg%d' % (tb % 2), W=[('stage', tb % 2)])
        for cg in range(4):
            b = pb % 4; pb += 1
            for i in range(4):
                c = cg * 4 + i
                P.op('pe', lambda e, b=b, i=i, c=c, st=st, rows=rows: e.transpose(bank(b)[:, i * 128:i * 128 + rows], st[0:rows, c * 128:(c + 1) * 128], identf[0:rows, 0:rows]),
                     R=[('stage', tb % 2), 'identf'], W=[('ps', b)])
            evac(x3[:, cg * 4:cg * 4 + 4, tb * 128:tb * 128 + rows], bank(b).rearrange("p (i n) -> p i n", i=4)[:, :, 0:rows],
                 R=[('ps', b)], W=K('x', range(cg * 4, cg * 4 + 4), tb * 128, tb * 128 + rows))

    barrier([('stage', 0), ('stage', 1)], ['r3go'])

    def rmsnorm(gi, a, b, out3, okey, ocol0):
        n = b - a
        for c in range(NCH):
            sq = nsq[c % 2]
            P.op('act', lambda e, c=c, sq=sq: e.activation(out=sq[:, 0:n], in_=x3[:, c, a:b], func=AF.Square),
                 R=K('x', [c], a, b), W=[('nsq', c % 2)])
            if c == 0:
                P.op('dve', lambda e, sq=sq: e.tensor_copy(out=nacc[:, 0:n], in_=sq[:, 0:n]), R=[('nsq', 0)], W=['nacc'])
            else:
                P.op('dve', lambda e, sq=sq: e.tensor_tensor(out=nacc[:, 0:n], in0=nacc[:, 0:n], in1=sq[:, 0:n], op=ALU.add),
                     R=[('nsq', c % 2), 'nacc'], W=['nacc'])
        hi = nhl[:, 0:n]; lo = nhl[:, 448:448 + n]
        P.op('dve', lambda e: e.tensor_copy(out=hi, in_=nacc[:, 0:n]), R=['nacc'], W=['nhi'])
        P.op('dve', lambda e: e.tensor_tensor(out=lo, in0=nacc[:, 0:n], in1=hi, op=ALU.subtract), R=['nacc', 'nhi'], W=['nlo'])
        P.op('pe', lambda e: e.matmul(bank(6)[:, 0:n], lhsT=onesb[:, :], rhs=hi, start=True, stop=False), R=['onesb', 'nhi'], W=[('ps', 6)])
        P.op('pe', lambda e: e.matmul(bank(6)[:, 0:n], lhsT=onesb[:, :], rhs=lo, start=False, stop=True), R=['onesb', 'nlo'], W=[('ps', 6)])
        P.op('act', lambda e: e.activation(out=nrs[:, 0:n], in_=bank(6)[:, 0:n], func=AF.Sqrt, bias=epsb[:, 0:1], scale=1.0 / D),
             R=[('ps', 6), 'epsb'], W=['nrs'])
        P.op('dve', lambda e: e.reciprocal(out=nrstd[:, 0:n], in_=nrs[:, 0:n]), R=['nrs'], W=['nrstd'])
        for c in range(NCH):
            P.op('dve', lambda e, c=c: e.scalar_tensor_tensor(out=out3[:, c, a - ocol0:b - ocol0], in0=x3[:, c, a:b], scalar=gam[:, gi * 16 + c:gi * 16 + c + 1],
                                                               in1=nrstd[:, 0:n], op0=ALU.mult, op1=ALU.mult),
                 R=K('x', [c], a, b) + ['nrstd', 'gam'], W=K(okey, [c], a, b))

    def ffn(fi_, l, gi, c0):
        TILES = tiles_from(c0)
        wg, wu, wd = Wg[fi_][l], Wu[fi_][l], Wd[fi_][l]
        wgv = wg.rearrange("(k p) n -> p k n", p=128); wuv = wu.rearrange("(k p) n -> p k n", p=128)
        for (a, b) in TILES:
            rmsnorm(gi, a, b, xn3, 'xn', 0)
        dcnt = [0]

        def GU(gr):
            for fi in range(4):
                f = gr * 4 + fi
                sg = next_wslot(); su = next_wslot()
                P.dma('pool', wslot[sg], wgv[:, :, f * 128:(f + 1) * 128], 'w%d' % sg, W=[('w', sg)])
                P.dma('pool', wslot[su], wuv[:, :, f * 128:(f + 1) * 128], 'w%d' % su, W=[('w', su)])
                if gr >= 1:
                    DN_dma(gr - 1, fi)
                for t, (a, b) in enumerate(TILES):
                    n = b - a
                    for (s_, bk) in ((sg, 2 * t), (su, 2 * t + 1)):
                        for k in range(NCH):
                            P.op('pe', lambda e, s_=s_, bk=bk, k=k, a=a, b=b, n=n: e.matmul(bank(bk)[:, 0:n], lhsT=wslot[s_][:, k, :], rhs=xn3[:, k, a:b], start=(k == 0), stop=(k == 15)),
                                 R=[('w', s_)] + K('xn', [k], a, b), W=[('ps', bk)])
                    sl = sil[t % 2]
                    P.op('act', lambda e, t=t, n=n, sl=sl: e.activation(out=sl[:, 0:n], in_=bank(2 * t)[:, 0:n], func=AF.Silu),
                         R=[('ps', 2 * t)], W=[('sil', t % 2)])
                    P.op('dve', lambda e, t=t, n=n, sl=sl, fi=fi, a=a, b=b, gr=gr: e.tensor_tensor(out=Hb[gr % 2][:, fi, a:b], in0=sl[:, 0:n], in1=bank(2 * t + 1)[:, 0:n], op=ALU.mult),
                         R=[('sil', t % 2), ('ps', 2 * t + 1)], W=K(('H', gr % 2), [fi], a, b))

        def DN_dma(gr, q):
            wdv = wd[gr * 512:(gr + 1) * 512, :].rearrange("(f p) n -> p f n", p=128)
            hm, hq = q // 2, q % 2
            P.dma('pool', Wdb[hm][:, :, hq * 512:(hq + 1) * 512], wdv[:, :, q * 512:(q + 1) * 512], 'wd%d' % q, R=['r3go'], W=[('wd', q)])

        def DN(gr):
            for m in range(NCH):
                hm = m // 8
                for t, (a, b) in enumerate(TILES):
                    n = b - a
                    bk = (6, 7, 4, 5)[dcnt[0] % 4]; dcnt[0] += 1
                    for fi in range(4):
                        P.op('pe', lambda e, bk=bk, fi=fi, m=m, hm=hm, a=a, b=b, n=n, gr=gr: e.matmul(bank(bk)[:, 0:n], lhsT=Wdb[hm][:, fi, (m % 8) * 128:(m % 8 + 1) * 128], rhs=Hb[gr % 2][:, fi, a:b], start=(fi == 0), stop=(fi == 3)),
                             R=[('wd', m // 4)] + K(('H', gr % 2), [fi], a, b), W=[('ps', bk)])
                    P.op('dve', lambda e, bk=bk, m=m, a=a, b=b, n=n: e.scalar_tensor_tensor(out=x3[:, m, a:b], in0=bank(bk)[:, 0:n], scalar=0.5, in1=x3[:, m, a:b], op0=ALU.mult, op1=ALU.add),
                         R=[('ps', bk)] + K('x', [m], a, b), W=K('x', [m], a, b))

        NG = NF // 4
        GU(0)
        for gr in range(1, NG):
            GU(gr)
            DN(gr - 1)
        for q in range(4):
            DN_dma(NG - 1, q)
        DN(NG - 1)

    def mixer(l):
        gi = 3 * l + 1
        winv = w_in[l].rearrange("(k p) n -> p k n", p=128)
        otmp = [r2(TM + i * 1024, 1024, F32) for i in range(2)]
        oc = [0]
        pbk = [0]

        def out_tok(psrc_fn, ncols, dst_fn):
            i = oc[0] % 2; oc[0] += 1
            ot = otmp[i]
            return i, ot

        barrier(ALLXN, ['mixgo'])
        MT = TILES if l == 0 else [(128, 512), (512, 896), (896, 1344)]
        for t, (a, b) in enumerate(MT):
            n = b - a
            rmsnorm(gi, a, b, xnt3, 'xnt', a)
            for cb in range(16):
                s_ = next_wslot()
                P.dma('pool', wslot[s_], winv[:, :, cb * 128:(cb + 1) * 128], 'w%d' % s_, W=[('w', s_)])
                bk = pbk[0] % 6; pbk[0] += 1
                for k in range(NCH):
                    P.op('pe', lambda e, s_=s_, bk=bk, k=k, n=n: e.matmul(bank(bk)[:, 0:n], lhsT=wslot[s_][:, k, :], rhs=xnt3[:, k, 0:n], start=(k == 0), stop=(k == 15)),
                         R=[('w', s_)] + K('xnt', [k], a, b), W=[('ps', bk)])
                evac(mix3[:, cb, a:b], bank(bk)[:, 0:n], R=[('ps', bk)], W=K('mix', [cb], a, b))
                if b == NT and cb < 8 and 'noutok' not in KFLAGS:
                    for (ca, cbb, rows, kind) in ((1152, 1280, 128, 'p'), (1280, 1344, 64, 's')):
                        bk2 = 6 + oc[0] % 2; i = oc[0] % 2; oc[0] += 1
                        for k in range(NCH):
                            P.op('pe', lambda e, s_=s_, bk2=bk2, k=k, ca=ca, cbb=cbb, rows=rows, a=a: e.matmul(bank(bk2)[0:rows, 0:128], lhsT=xnt3[:, k, ca - a:cbb - a], rhs=wslot[s_][:, k, :], start=(k == 0), stop=(k == 15)),
                                 R=[('w', s_)] + K('xnt', [k], ca, cbb), W=[('ps', bk2)])
                        ot = otmp[i]
                        P.op('dve', lambda e, bk2=bk2, rows=rows, ot=ot: e.tensor_copy(out=ot[0:rows, 0:128], in_=bank(bk2)[0:rows, 0:128]), R=[('ps', bk2)], W=[('otmp', i)])
                        if kind == 'p':
                            P.dma('sp', npp[l, :, cb * 128:(cb + 1) * 128], ot[113:128, 0:128], 'o%d' % i, R=[('otmp', i)], is_out=True)
                        else:
                            P.dma('sp', nps[l, :, 11:15, cb * 128:(cb + 1) * 128], ot[0:64, 0:128], 'o%d' % i, R=[('otmp', i)], is_out=True)
            kslots = []
            for g in range(4):
                s_ = next_wslot(); kslots.append(s_)
                P.dma('pool', wslot[s_][:, :, 0:64], winv[:, :, 2048 + g * 64:2048 + (g + 1) * 64], 'w%d' % s_, W=[('w', s_)])
                P.op('act', lambda e, s_=s_: e.activation(out=wslot[s_][:, :, 64:128], in_=wslot[s_][:, :, 0:64], func=AF.Copy), R=[('w', s_)], W=[('w', s_)])
                bk = pbk[0] % 6; pbk[0] += 1
                for k in range(NCH):
                    P.op('pe', lambda e, s_=s_, bk=bk, k=k, n=n: e.matmul(bank(bk)[:, 0:n], lhsT=wslot[s_][:, k, :], rhs=xnt3[:, k, 0:n], start=(k == 0), stop=(k == 15)),
                         R=[('w', s_)] + K('xnt', [k], a, b), W=[('ps', bk)])
                evac(kT3[:, g, a:b], bank(bk)[:, 0:n], R=[('ps', bk)], W=K('kT', [g], a, b))
            if b == NT and 'noktok' not in KFLAGS:
                for (ca, cbb, rows, kind) in ((1152, 1280, 128, 'p'), (1280, 1344, 64, 's')):
                    bk2 = 6 + oc[0] % 2; i = oc[0] % 2; oc[0] += 1
                    for g in range(4):
                        for k in range(NCH):
                            P.op('pe', lambda e, g=g, bk2=bk2, k=k, ca=ca, cbb=cbb, rows=rows, a=a, kslots=kslots: e.matmul(bank(bk2)[0:rows, g * 64:(g + 1) * 64], lhsT=xnt3[:, k, ca - a:cbb - a], rhs=wslot[kslots[g]][:, k, 0:64], start=(k == 0), stop=(k == 15)),
                                 R=[('w', kslots[g])] + K('xnt', [k], ca, cbb), W=[('ps', bk2)])
                    ot = otmp[i]
                    P.op('dve', lambda e, bk2=bk2, rows=rows, ot=ot: e.tensor_copy(out=ot[0:rows, :], in_=bank(bk2)[0:rows, 0:256]), R=[('ps', bk2)], W=[('otmp', i)])
                    if kind == 'p':
                        P.dma('sp', nkp[l], ot[:, :], 'o%d' % i, R=[('otmp', i)], is_out=True)
                    else:
                        P.dma('sp', nks[l, :, 124:128, :], ot[0:64, :], 'o%d' % i, R=[('otmp', i)], is_out=True)
            sv = [next_wslot(), next_wslot()]
            for j in range(2):
                P.dma('pool', wslot[sv[j]], winv[:, :, 2304 + j * 128:2304 + (j + 1) * 128], 'w%d' % sv[j], W=[('w', sv[j])])
            c0 = a if 'nov' not in KFLAGS else b
            while c0 < b:
                blk = c0 // 128
                c1 = min(b, (blk + 1) * 128)
                rows = c1 - c0; po = c0 - blk * 128
                bk = pbk[0] % 6; pbk[0] += 1
                for j in range(2):
                    for k in range(NCH):
                        P.op('pe', lambda e, j=j, bk=bk, k=k, c0=c0, c1=c1, rows=rows, po=po, a=a, sv=sv: e.matmul(bank(bk)[po:po + rows, j * 128:(j + 1) * 128], lhsT=xnt3[:, k, c0 - a:c1 - a], rhs=wslot[sv[j]][:, k, :], start=(k == 0), stop=(k == 15)),
                             R=[('w', sv[j])] + K('xnt', [k], c0, c1), W=[('ps', bk)])
                if blk >= 9:
                    P.op('dve', lambda e, bk=bk, rows=rows, po=po, blk=blk: e.tensor_copy(out=vtok[po:po + rows, blk, :], in_=bank(bk)[po:po + rows, 0:256]), R=[('ps', bk)], W=K('vtok', [blk], c0, c1))
                else:
                    evac(vtok[po:po + rows, blk, :], bank(bk)[po:po + rows, 0:256], R=[('ps', bk)], W=K('vtok', [blk], c0, c1))
                if blk >= 9 and 'novout' not in KFLAGS and not (blk == 10 and 'novs' in KFLAGS) and not (blk == 9 and 'novp' in KFLAGS):
                    i = oc[0] % 2; oc[0] += 1
                    ot = otmp[i]
                    P.op('dve', lambda e, bk=bk, rows=rows, po=po, ot=ot: e.tensor_copy(out=ot[po:po + rows, :], in_=bank(bk)[po:po + rows, 0:256]), R=[('ps', bk)], W=[('otmp', i)])
                    if blk == 9:
                        P.dma('sp', nvp[l], ot[:, :], 'o%d' % i, R=[('otmp', i)], is_out=True)
                    else:
                        P.dma('sp', nvs[l, :, 124:128, :], ot[0:64, :], 'o%d' % i, R=[('otmp', i)], is_out=True)
                c0 = c1
        if 'nopass' in KFLAGS:
            return
        P.dma('sp', nks[l, :, 0:124, :], ck[l, :, 4:128, :], 'oc', is_out=True)
        P.dma('sp', nvs[l, :, 0:124, :], cv[l, :, 4:128, :], 'oc', is_out=True)
        P.dma('sp', nps[l, :, 0:11, :], spool[l, :, 4:15, :], 'oc', is_out=True)

        barrier(K('xnt', range(NCH), 0, NT), ['m3go'])
        barrier([], ['m2go', ('w', 0), ('w', 1), ('w', 2), ('w', 3)])
        if MSTOP < 2:
            return
        def r4v(off, nbytes):
            return R4[:, off // 2:(off + nbytes) // 2].bitcast(F32)
        WSK = [('w', 0), ('w', 1), ('w', 2), ('w', 3)]
        def m2_gen():
            bufA = r4v(0, 5120); bufB = r4v(5120, 5120)
            hs = r4v(10240, 2432).rearrange("p (c s j) -> p c s j", c=2, s=16)
            dbuf = r2(TM + 2048, 5376).rearrange("p (c n) -> p c n", c=2)
            ststage = r2(TM + 2048 + 5376, 512, F32)
            pw = r2(TM + 2048 + 5376 + 512, 4096).rearrange("p (g k n) -> p g k n", g=4, k=2)
            P.dma('pool', pw, pool_w[l].rearrange("g (k p) n -> p g k n", p=128), 'pw', W=['pw'])
            for g in range(4):
                w = POOLW[g]
                for ci in range(2):
                    c = 2 * g + ci
                    u = mix3[:, c, 0:SC]
                    src = None
                    bufs = [bufA, bufB]
                    cur = u; sh = 1; bi = 0
                    for step in range(g + 1):
                        dst = bufs[bi]
                        P.op('pool', lambda e, dst=dst, cur=cur, sh=sh: e.tensor_tensor(out=dst[:, sh:SC], in0=cur[:, sh:SC], in1=cur[:, 0:SC - sh], op=ALU.add),
                             R=K('mix', [c], 0, SC) + [('pbuf', 1 - bi), 'm2go'], W=[('pbuf', bi)])
                        P.op('pool', lambda e, dst=dst, cur=cur, sh=sh: e.tensor_copy(out=dst[:, 0:sh], in_=cur[:, 0:sh]),
                             R=K('mix', [c], 0, SC) + [('pbuf', 1 - bi), 'm2go'], W=[('pbuf', bi)])
                        cur = dst; sh *= 2; bi = 1 - bi
                    fb = 1 - bi
                    P.op('dve', lambda e, cur=cur, u=u, ci=ci, w=w: e.scalar_tensor_tensor(out=dbuf[:, ci, 0:SC], in0=cur[:, 0:SC], scalar=1.0 / w, in1=u, op0=ALU.mult, op1=ALU.subtract),
                         R=[('pbuf', fb)] + K('mix', [c], 0, SC), W=K('dbuf', [ci], 0, SC))
                    P.op('dve', lambda e, cur=cur, g=g: e.tensor_tensor(out=ntmp[:, 64:80], in0=cur[:, 256:272], in1=invc[:, g * 16:(g + 1) * 16], op=ALU.mult),
                         R=[('pbuf', fb), 'invc', ('nsq', 0)], W=[('nsq', 0)])
                    P.op('dve', lambda e, u=u, ci=ci: e.tensor_tensor(out=dbuf[:, ci, 256:272], in0=ntmp[:, 64:80], in1=u[:, 256:272], op=ALU.subtract),
                         R=[('nsq', 0)] + K('mix', [c], 256, 272), W=K('dbuf', [ci], 256, 272))
                    for bt in range(2):
                        P.dma('sp', ststage[0:120, :], spool[l, bt * 8:(bt + 1) * 8, :, c * 128:(c + 1) * 128].rearrange("b r n -> (b r) n"), 'sst', W=['ststage'])
                        P.op('pe', lambda e: e.transpose(bank(6)[:, 0:120], ststage[0:120, :], identf[0:120, 0:120]), R=['ststage', 'identf'], W=[('ps', 6)])
                        P.op('dve', lambda e, ci=ci, bt=bt: e.tensor_copy(out=hs[:, ci, bt * 8:(bt + 1) * 8, 0:15], in_=bank(6)[:, 0:120].rearrange("p (s j) -> p s j", s=8)),
                             R=[('ps', 6)], W=[('hs', ci)])
                    P.op('dve', lambda e, ci=ci, c=c: e.tensor_copy(out=hs[:, ci, :, 15:19], in_=mix3[:, c, SC:NT].rearrange("p (s t) -> p s t", t=4)),
                         R=K('mix', [c], SC, NT), W=[('hs', ci)])
                    for t4 in range(4):
                        P.op('dve', lambda e, ci=ci, t4=t4, w=w: e.tensor_reduce(out=bufA[:, 1280 - 64 + t4 * 16:1280 - 64 + (t4 + 1) * 16] if False else ntmp[:, t4 * 16:(t4 + 1) * 16], in_=hs[:, ci, :, 16 + t4 - w:16 + t4], op=ALU.add, axis=AX.X),
                             R=[('hs', ci), ('nsq', 0)], W=['psum4', ('nsq', 0)])
                    P.op('dve', lambda e, ci=ci, c=c, w=w: e.scalar_tensor_tensor(out=dbuf[:, ci, SC:NT].rearrange("p (s t) -> p s t", t=4), in0=ntmp[:, 0:64].rearrange("p (t s) -> p s t", t=4), scalar=1.0 / w,
                                                                                 in1=mix3[:, c, SC:NT].rearrange("p (s t) -> p s t", t=4), op0=ALU.mult, op1=ALU.subtract),
                         R=['psum4'] + K('mix', [c], SC, NT), W=K('dbuf', [ci], SC, NT))
                    yield
                for mo in range(2):
                    c = 2 * g + mo
                    for t, (a, b) in enumerate(TILES):
                        n = b - a
                        bk = pbk[0] % 6; pbk[0] += 1
                        for ki in range(2):
                            P.op('pe', lambda e, bk=bk, ki=ki, mo=mo, g=g, a=a, b=b, n=n: e.matmul(bank(bk)[:, 0:n], lhsT=pw[:, g, ki, mo * 128:(mo + 1) * 128], rhs=dbuf[:, ki, a:b], start=(ki == 0), stop=(ki == 1)),
                                 R=['pw'] + K('dbuf', [ki], a, b), W=[('ps', bk)])
                        P.op('act', lambda e, bk=bk, c=c, a=a, b=b, n=n: e.activation(out=mix3[:, c, a:b], in_=bank(bk)[:, 0:n], func=AF.Copy, scale=pscale[:, l * 8 + c:l * 8 + c + 1]),
                             R=[('ps', bk), 'pscale'] + K('dbuf', [0, 1], a, b), W=K('mix', [c], a, b))
                yield

        if MSTOP < 3:
            for _ in m2_gen():
                pass
            return
        XB = 0
        pbuf = [r2(XB + i * 2048, 2048).rearrange("p (h n) -> p h n", h=4) for i in range(2)]
        ptsb = [r2(XB + 4096 + i * 2048, 2048).rearrange("p (j n) -> p j n", j=8) for i in range(2)]
        onb = [r2(XB + 8192 + i * 512, 512).rearrange("p (h n) -> p h n", h=4) for i in range(2)]
        sm = [r2(XB + 9216 + i * 128, 128, F32) for i in range(4)]
        def smv(j):
            smi = sm[j]
            return smi[:, 0:4], smi[:, 4:8], smi[:, 8:12], smi[:, 12:16], smi[:, 16:20], smi[:, 20:24]

        def stA(n_, blk, g):
            i = n_ % 2; j = n_ % 4
            Sb = (0, 1) if i == 0 else (2, 3)
            mk = masks[:, 0:256] if blk == 2 else masks[:, 256:512]
            S4 = (PSA[:, 0:1024] if i == 0 else PSA[:, 1024:2048]).rearrange("p (h n) -> p h n", h=4)
            for h in range(4):
                po = (h % 2) * 64
                reg = bank(Sb[h // 2])[:, (h % 2) * 256:(h % 2 + 1) * 256]
                P.op('pe', lambda e, reg=reg, mk=mk: e.matmul(reg, lhsT=identb[:, :], rhs=mk, start=True, stop=False),
                     R=['identb', 'masks'], W=[('ps', Sb[h // 2])])
                P.op('pe', lambda e, reg=reg, po=po, g=g, h=h, blk=blk: e.matmul(reg, lhsT=mix3[po:po + 64, 8 + 2 * g + h // 2, blk * 128:(blk + 1) * 128],
                                                                                  rhs=kT3[po:po + 64, g, (blk - 1) * 128:(blk + 1) * 128], start=False, stop=True),
                     R=K('mix', [8 + 2 * g + h // 2], blk * 128, (blk + 1) * 128) + K('kT', [g], (blk - 1) * 128, (blk + 1) * 128), W=[('ps', Sb[h // 2])])
            rmax, negm, rsum, dd, esv, rinv = smv(j)
            nsg = nsink[:, l * 16 + 4 * g:l * 16 + 4 * g + 4]
            P.op('dve', lambda e, S4=S4, rmax=rmax: e.reduce_max(out=rmax, in_=S4, axis=AX.X), R=[('ps', Sb[0]), ('ps', Sb[1])], W=[('sm', j, 0)])
            P.op('dve', lambda e, rmax=rmax, negm=negm, nsg=nsg: e.scalar_tensor_tensor(out=negm, in0=rmax, scalar=-SCALE, in1=nsg, op0=ALU.mult, op1=ALU.min), R=[('sm', j, 0), 'nsink'], W=[('sm', j, 1)])
            P.op('dve', lambda e, dd=dd, negm=negm, nsg=nsg: e.tensor_tensor(out=dd, in0=negm, in1=nsg, op=ALU.subtract), R=[('sm', j, 1), 'nsink'], W=[('sm', j, 3)])
            for h in range(4):
                P.op('act', lambda e, h=h, S4=S4, negm=negm, i=i: e.activation(out=pbuf[i][:, h, :], in_=S4[:, h, :], func=AF.Exp, bias=negm[:, h:h + 1], scale=SCALE),
                     R=[('ps', Sb[h // 2]), ('sm', j, 1)], W=[('p', i)])
            P.op('act', lambda e, dd=dd, esv=esv: e.activation(out=esv, in_=dd, func=AF.Exp), R=[('sm', j, 3)], W=[('sm', j, 4)])
            P.op('dve', lambda e, i=i, rsum=rsum: e.reduce_sum(out=rsum, in_=pbuf[i], axis=AX.X), R=[('p', i)], W=[('sm', j, 2)])
            P.op('dve', lambda e, rsum=rsum, esv=esv, rinv=rinv: e.tensor_tensor(out=rinv, in0=rsum, in1=esv, op=ALU.add), R=[('sm', j, 2), ('sm', j, 4)], W=[('sm', j, 5)])
            P.op('dve', lambda e, rinv=rinv: e.reciprocal(out=rinv, in_=rinv), R=[('sm', j, 5)], W=[('sm', j, 5)])

        def stB(n_, blk, g):
            i = n_ % 2
            ptb = 4 + i
            for h in range(4):
                for kb in range(2):
                    P.op('pe', lambda e, h=h, kb=kb, i=i, ptb=ptb: e.transpose(bankb(ptb)[:, (h * 2 + kb) * 128:(h * 2 + kb + 1) * 128], pbuf[i][:, h, kb * 128:(kb + 1) * 128], identb[:, :]),
                         R=[('p', i), 'identb'], W=[('ps', ptb)])
            P.op('act', lambda e, i=i, ptb=ptb: e.activation(out=ptsb[i], in_=bankb(ptb).rearrange("p (j n) -> p j n", j=8), func=AF.Copy), R=[('ps', ptb)], W=[('pt', i)])

        def stC(n_, blk, g):
            i = n_ % 2; j = n_ % 4
            rinv = smv(j)[5]
            ob = 6 + i
            for h in range(4):
                for kb in range(2):
                    P.op('pe', lambda e, h=h, kb=kb, i=i, ob=ob, g=g, blk=blk: e.matmul(bank(ob)[:, h * 64:(h + 1) * 64], lhsT=ptsb[i][:, h * 2 + kb, :], rhs=vtok[:, blk - 1 + kb, g * 64:(g + 1) * 64], start=(kb == 0), stop=(kb == 1)),
                         R=[('pt', i)] + K('vtok', [blk - 1 + kb], (blk - 1 + kb) * 128, (blk + kb) * 128), W=[('ps', ob)])
            P.op('dve', lambda e, i=i, ob=ob, rinv=rinv: e.tensor_tensor(out=onb[i], in0=bank(ob)[:, 0:256].rearrange("p (h n) -> p h n", h=4), in1=rinv.unsqueeze(2).broadcast_to([128, 4, 64]), op=ALU.mult),
                 R=[('ps', ob), ('sm', j, 5)], W=[('on', i)])

        def stD(n_, blk, g):
            i = n_ % 2
            ob = 6 + i
            otv = bank(ob)[:, 256:384].bitcast(BF16)
            for hh in range(2):
                P.op('pe', lambda e, hh=hh, i=i, otv=otv: e.transpose(otv[:, hh * 128:(hh + 1) * 128], onb[i][:, 2 * hh:2 * hh + 2, :].rearrange("p h n -> p (h n)"), identb[:, :]),
                     R=[('on', i), 'identb'], W=[('ps', ob)])
            P.op('act', lambda e, g=g, blk=blk, otv=otv: e.activation(out=mix3[:, 8 + 2 * g:8 + 2 * g + 2, blk * 128:(blk + 1) * 128], in_=otv.rearrange("p (h n) -> p h n", h=2), func=AF.Copy),
                 R=[('ps', ob)], W=K('mix', [8 + 2 * g, 8 + 2 * g + 1], blk * 128, (blk + 1) * 128))

        unitsl = [(blk, g) for blk in range(1 + l, 10) for g in range(4)]
        NU = len(unitsl)
        m2g = m2_gen()
        for k in range(NU + 3):
            for st, off in ((stA, 0), (stB, 1), (stC, 2), (stD, 3)):
                n_ = k - off
                if 0 <= n_ < NU:
                    st(n_, unitsl[n_][0], unitsl[n_][1])
            if k % 3 == 2:
                next(m2g, None)
        for _ in m2g:
            pass
        barrier([('pbuf', 0), ('pbuf', 1), ('hs', 0), ('hs', 1)], [('w', 0), ('w', 1), ('w', 2), ('w', 3)])

        if MSTOP < 4:
            return
        kc = r2(XB, 2048).rearrange("p (s u d) -> p s u d", s=8, u=2)
        vc = r2(XB + 2048, 4096).rearrange("p (s n) -> p s n", s=8)
        kcT = r2(XB + 6144, 2048).rearrange("p (s n) -> p s n", s=8)
        ps_ = r2(XB + 8192, 2176)
        pts = r2(XB + 10368, 2048).rearrange("p (s n) -> p s n", s=8)
        ptn = r2(XB + 12416, 256)
        onA = r2(XB + 12672, 256); onB = r2(XB + 12928, 256)
        sms = r2(XB + 13184, 128, F32)
        qs = r2(XB + 13312, 128)
        M3K = [('on', 0), ('on', 1), ('p', 0), ('p', 1), ('pt', 0), ('pt', 1)] + [('sm', i_, j_) for i_ in range(4) for j_ in range(6)]
        barrier(M3K + [('pbuf', 0), ('pbuf', 1), ('hs', 0), ('hs', 1)], ['m4go'])
        P.op('dve', lambda e: e.memset(onA[:, :], 0.0), W=['onA'])
        P.op('dve', lambda e: e.memset(onB[:, :], 0.0), W=['onB'])
        S = PSA[:, 0:1088]
        for bt in range(2):
            P.dma('pool', vc, cv[l, bt * 8:(bt + 1) * 8, :, :].rearrange("s k n -> k s n"), 'vc', R=['m4go'], W=['vc'])
            for g in range(4):
                for dup in range(2):
                    P.dma('pool', kc[:, :, dup, :], ck[l, bt * 8:(bt + 1) * 8, :, g * 64:(g + 1) * 64].rearrange("s k d -> k s d"), 'kc', R=['m4go'], W=['kc'])
                for s8 in range(8):
                    P.op('pe', lambda e, s8=s8: e.transpose(bankb(3)[:, s8 * 128:(s8 + 1) * 128], kc[:, s8, :, :].rearrange("p u d -> p (u d)"), identb[:, :]),
                         R=['kc', 'identb'], W=[('ps', 3)])
                P.op('act', lambda e: e.activation(out=kcT, in_=bankb(3).rearrange("p (s n) -> p s n", s=8), func=AF.Copy), R=[('ps', 3)], W=['kcT'])
                qcols = slice(SC + bt * 32, SC + bt * 32 + 32)
                P.op('dve', lambda e, g=g, qcols=qcols: e.tensor_copy(out=qs[:, :].rearrange("p (h n) -> p h n", h=2), in_=mix3[:, 8 + 2 * g:8 + 2 * g + 2, qcols]),
                     R=K('mix', [8 + 2 * g, 8 + 2 * g + 1], SC, NT), W=['qs'])
                for hp in range(2):
                    po = hp * 64
                    lh = qs[po:po + 64, :]
                    for s8 in range(9):
                        if s8 < 8:
                            reg = S[po:po + 64, s8 * 128:(s8 + 1) * 128]; mk = smask[:, s8 * 128:(s8 + 1) * 128]; rh = kcT[po:po + 64, s8, :]
                            RR = ['kcT', 'qs']
                        else:
                            reg = S[po:po + 64, 1024:1088]; mk = smask[:, 1024 + bt * 64:1088 + bt * 64]; rh = kT3[po:po + 64, g, SC:NT]
                            RR = K('kT', [g], SC, NT) + ['qs']
                        bkk = min(s8 // 4, 2)
                        P.op('pe', lambda e, reg=reg, mk=mk, po=po: e.matmul(reg, lhsT=identb[:, po:po + 64], rhs=mk, start=True, stop=False),
                             R=['identb', 'smask'], W=[('ps', bkk)])
                        P.op('pe', lambda e, reg=reg, lh=lh, rh=rh: e.matmul(reg, lhsT=lh, rhs=rh, start=False, stop=True),
                             R=RR, W=[('ps', bkk)])
                rmax = sms[:, 0:1]; negm = sms[:, 1:2]; rsum = sms[:, 2:3]; dd = sms[:, 3:4]; esv = sms[:, 4:5]; rinv = sms[:, 5:6]
                nsg = nsrow[:, l * 4 + g:l * 4 + g + 1]
                SK = [('ps', 0), ('ps', 1), ('ps', 2)]
                P.op('dve', lambda e, rmax=rmax: e.reduce_max(out=rmax, in_=S, axis=AX.X), R=SK, W=['sms0'])
                P.op('dve', lambda e, rmax=rmax, negm=negm: e.tensor_scalar(out=negm, in0=rmax, scalar1=-SCALE, scalar2=None, op0=ALU.mult), R=['sms0'], W=['sms1'])
                P.op('dve', lambda e, negm=negm, nsg=nsg: e.tensor_tensor(out=negm, in0=negm, in1=nsg, op=ALU.min), R=['sms1', 'nsrow'], W=['sms1'])
                P.op('act', lambda e, negm=negm: e.activation(out=ps_, in_=S, func=AF.Exp, bias=negm, scale=SCALE), R=SK + ['sms1'], W=['ps_'])
                P.op('dve', lambda e, rsum=rsum: e.reduce_sum(out=rsum, in_=ps_, axis=AX.X), R=['ps_'], W=['sms2'])
                P.op('dve', lambda e, dd=dd, negm=negm, nsg=nsg: e.tensor_tensor(out=dd, in0=negm, in1=nsg, op=ALU.subtract), R=['sms1', 'nsrow'], W=['sms3'])
                P.op('act', lambda e, dd=dd, esv=esv: e.activation(out=esv, in_=dd, func=AF.Exp), R=['sms3'], W=['sms4'])
                P.op('dve', lambda e, rsum=rsum, esv=esv, rinv=rinv: e.tensor_tensor(out=rinv, in0=rsum, in1=esv, op=ALU.add), R=['sms2', 'sms4'], W=['sms5'])
                P.op('dve', lambda e, rinv=rinv: e.reciprocal(out=rinv, in_=rinv), R=['sms5'], W=['sms5'])
                for s8 in range(8):
                    P.op('pe', lambda e, s8=s8: e.transpose(bankb(4)[:, s8 * 128:(s8 + 1) * 128], ps_[:, s8 * 128:(s8 + 1) * 128], identb[:, :]), R=['ps_', 'identb'], W=[('ps', 4)])
                P.op('pe', lambda e: e.transpose(bankb(5)[0:64, 0:128], ps_[:, 1024:1088], identb[:, :]), R=['ps_', 'identb'], W=[('ps', 5)])
                P.op('act', lambda e: e.activation(out=pts, in_=bankb(4).rearrange("p (s n) -> p s n", s=8), func=AF.Copy), R=[('ps', 4)], W=['pts'])
                P.op('dve', lambda e: e.tensor_copy(out=ptn[0:64, :], in_=bankb(5)[0:64, 0:128]), R=[('ps', 5)], W=['ptn'])
                for s8 in range(8):
                    P.op('pe', lambda e, s8=s8, g=g: e.matmul(bank(6)[:, 0:64], lhsT=pts[:, s8, :], rhs=vc[:, s8, g * 64:(g + 1) * 64], start=(s8 == 0), stop=False),
                         R=['pts', 'vc'], W=[('ps', 6)])
                P.op('pe', lambda e, g=g: e.matmul(bank(6)[:, 0:64], lhsT=ptn[0:64, :], rhs=vtok[0:64, 10, g * 64:(g + 1) * 64], start=False, stop=True),
                     R=['ptn'] + K('vtok', [10], SC, NT), W=[('ps', 6)])
                P.op('dve', lambda e, rinv=rinv: e.tensor_scalar(out=onA[:, 0:64], in0=bank(6)[:, 0:64], scalar1=rinv, scalar2=None, op0=ALU.mult), R=[('ps', 6), 'sms5'], W=['onA'])
                P.op('dve', lambda e, rinv=rinv: e.tensor_scalar(out=onB[:, 64:128], in0=bank(6)[:, 0:64], scalar1=rinv, scalar2=None, op0=ALU.mult), R=[('ps', 6), 'sms5'], W=['onB'])
                for hh in range(2):
                    P.op('pe', lambda e, hh=hh: e.matmul(bank(7)[:, hh * 32:(hh + 1) * 32], lhsT=onA[:, :], rhs=sel[:, (hh * 2 + 0) * 32:(hh * 2 + 1) * 32], start=True, stop=False),
                         R=['onA', 'sel'], W=[('ps', 7)])
                    P.op('pe', lambda e, hh=hh: e.matmul(bank(7)[:, hh * 32:(hh + 1) * 32], lhsT=onB[:, :], rhs=sel[:, (hh * 2 + 1) * 32:(hh * 2 + 2) * 32], start=False, stop=True),
                         R=['onB', 'sel'], W=[('ps', 7)])
                P.op('act', lambda e, g=g, qcols=qcols: e.activation(out=mix3[:, 8 + 2 * g:8 + 2 * g + 2, qcols], in_=bank(7)[:, 0:64].rearrange("p (h n) -> p h n", h=2), func=AF.Copy),
                     R=[('ps', 7)], W=K('mix', [8 + 2 * g, 8 + 2 * g + 1], SC, NT))

        if MSTOP < 5:
            return
        wov = w_out[l].rearrange("(k p) n -> p k n", p=128)
        dc = 0
        for mb in range(NCH):
            s_ = next_wslot()
            P.dma('pool', wslot[s_], wov[:, :, mb * 128:(mb + 1) * 128], 'w%d' % s_, W=[('w', s_)])
            for t, (a, b) in enumerate(tiles_from(128 * (l + 1))):
                n = b - a
                bk = dc % 6; dc += 1
                for k in range(NCH):
                    P.op('pe', lambda e, s_=s_, bk=bk, k=k, a=a, b=b, n=n: e.matmul(bank(bk)[:, 0:n], lhsT=wslot[s_][:, k, :], rhs=mix3[:, k, a:b], start=(k == 0), stop=(k == 15)),
                         R=[('w', s_)] + K('mix', [k], a, b), W=[('ps', bk)])
                P.op('dve', lambda e, bk=bk, mb=mb, a=a, b=b, n=n: e.tensor_tensor(out=x3[:, mb, a:b], in0=bank(bk)[:, 0:n], in1=x3[:, mb, a:b], op=ALU.add),
                     R=[('ps', bk)] + K('x', [mb], a, b), W=K('x', [mb], a, b))

    MIXK = ['qs', 'kc', 'vc', 'kcT', 'ps_', 'pts', 'ptn', 'onA', 'onB', 'pw', 'ststage', ('otmp', 0), ('otmp', 1)] + K('kT', range(4), 0, NT) + K('vtok', range(11), 0, 128) + K('dbuf', range(2), 0, NT) + K('xnt', range(NCH), 0, NT)

    ph = 0
    for l in range(2):
        for fi_, fn_ in enumerate((lambda: ffn(0, l, 3 * l + 0, 128 * l), lambda: (mixer(l), barrier(MIXK + ALLMIX, ['r3go'])), lambda: ffn(1, l, 3 * l + 2, 128 * (l + 1)))):
            if ph < STOP_AFTER and not ('noffn' in KFLAGS and fi_ != 1):
                fn_()
            ph += 1

    for (a, b) in tiles_from(256):
        if 'nonorm' not in KFLAGS:
            rmsnorm(6, a, b, x3, 'x', 0)
    pb = 0
    for tb in (range(2, 11) if 'nofinal' not in KFLAGS else []):
        rows = 128 if tb < 10 else 64
        st = stage[tb % 2]
        for cg in range(4):
            b = pb % 4; pb += 1
            for i in range(4):
                c = cg * 4 + i
                P.op('pe', lambda e, b=b, i=i, c=c, rows=rows, tb=tb: e.transpose(bank(b)[0:rows, i * 128:(i + 1) * 128], x3[:, c, tb * 128:tb * 128 + rows], identf[:, :]),
                     R=K('x', [c], tb * 128, tb * 128 + rows) + ['identf'], W=[('ps', b)])
            evac(st[0:rows, cg * 512:(cg + 1) * 512], bank(b)[0:rows, :], R=[('ps', b)], W=[('stage', tb % 2)])
        P.dma('sp', y[(tb - 2) * 128:(tb - 2) * 128 + rows, :], st[0:rows, :], 'stg%d' % (tb % 2), R=[('stage', tb % 2)], is_out=True)

    if 'dbgmix' in KFLAGS:
        dbg = dout("dbg", [128, 10752]); dbgx = dout("dbgx", [128, NCH * NT])
        P.dma('sp', dbg, R3[:, :].bitcast(F32), 'dbg', R=ALLMIX + [('stage', 0), ('stage', 1)], is_out=True)
        P.dma('sp', dbgx, R1[:, :], 'dbg', R=K('x', range(NCH), 0, NT), is_out=True)
    nops, nwaits = P.emit()
    return nc, es, nops, nwaits


_CACHE = {}


def _consts():
    identf = np.eye(128, dtype=np.float32)
    i = np.arange(128)[:, None]; j = np.arange(256)[None, :]
    rel = i + 128 - j
    band = ((rel >= 0) & (rel < 128))
    NEG = -30000.0
    maskB = np.where(band, 0.0, NEG).astype(np.float32)
    maskA0 = np.where(band & (j >= 128), 0.0, NEG).astype(np.float32)
    sm = np.full((2, 128, 1088), NEG, np.float32)
    sel = np.zeros((128, 2, 2, 32), np.float32)
    for r in range(128):
        hp = r // 64; hh = (r % 64) // 32; ii = (r % 32) // 4; t = r % 4
        sel[r, hh, hp, ii * 4 + t] = 1.0
        for bt in range(2):
            sm[bt, r, ii * 128 + t + 1:ii * 128 + 128] = 0.0
            sq = bt * 8 + ii
            sm[bt, r, 1024 + sq * 4:1024 + sq * 4 + t + 1] = 0.0
    return identf, maskA0, maskB, sm, sel.reshape(128, 128)


def kernel(x_prompt, x_sample, cache_k, cache_v, state_pool, norm_ffn1, ffn1_gate, ffn1_up, ffn1_down, norm_mix, w_in,
           pool_w, pool_scale, attn_sinks, w_out, norm_ffn2, ffn2_gate, ffn2_up, ffn2_down, final_norm):
    f32 = lambda a: np.ascontiguousarray(np.asarray(a, dtype=np.float32))
    x_prompt = f32(x_prompt); x_sample = f32(x_sample); cache_k = f32(cache_k); cache_v = f32(cache_v); state_pool = f32(state_pool)
    if 'nc' not in _CACHE:
        _CACHE['nc'] = build()
    nc = _CACHE['nc'][0]
    identf, maskA0, maskB, sm, sel = _consts()
    norms = np.stack([f32(norm_ffn1)[0], f32(norm_mix)[0], f32(norm_ffn2)[0], f32(norm_ffn1)[1], f32(norm_mix)[1], f32(norm_ffn2)[1], f32(final_norm)], 0)
    shared = dict(ffn1_gate=f32(ffn1_gate), ffn1_up=f32(ffn1_up), ffn1_down=f32(ffn1_down), ffn2_gate=f32(ffn2_gate), ffn2_up=f32(ffn2_up),
                  ffn2_down=f32(ffn2_down), w_in=f32(w_in), w_out=f32(w_out)) if not KSMALL else ({} if KSMALL == 1 else dict(w_in=f32(w_in), w_out=f32(w_out)))
    shared.update(norms=np.ascontiguousarray(norms), pool_w=f32(pool_w),
                  pool_scale=f32(pool_scale), attn_sinks=f32(attn_sinks), identf=identf, sel=sel)
    xs = x_sample.reshape(128 * 4, D)
    ckr = cache_k.reshape(2, 128, 128, 256); cvr = cache_v.reshape(2, 128, 128, 256)
    in_maps = []
    for c in range(8):
        bq, j = c // 4, c % 4
        xin = np.zeros((NT, D), np.float32)
        if j > 0:
            xin[0:256] = x_prompt[bq, j * 1024 - 256:j * 1024]
        xin[256:1280] = x_prompt[bq, j * 1024:(j + 1) * 1024]
        xin[1280:1344] = xs[c * 64:(c + 1) * 64]
        masks = np.concatenate([maskA0 if j == 0 else maskB, maskB], axis=1)
        invc = np.zeros((128, 4, 16), np.float32)
        for g, w in enumerate(POOLW):
            if j == 0:
                invc[:, g, :] = (1.0 / np.minimum(w, np.arange(16) + 1.0))[None, :]
            else:
                invc[:, g, :] = 1.0 / w
        m = dict(shared)
        m.update(xin=xin, ck=np.ascontiguousarray(ckr[:, c * 16:(c + 1) * 16]), cv=np.ascontiguousarray(cvr[:, c * 16:(c + 1) * 16]),
                 spool=np.ascontiguousarray(state_pool[:, c * 16:(c + 1) * 16]), masks=np.ascontiguousarray(masks),
                 smask=np.ascontiguousarray(np.concatenate([sm[0], sm[1][:, 1024:1088]], axis=1)), invc=np.ascontiguousarray(invc.reshape(128, 64)))
        in_maps.append(m)
    res = run_bass_kernel_spmd(nc, in_maps, core_ids=list(range(8)))
    R = res.results
    _CACHE["res"] = R
    y_prompt = np.zeros((2, 4096, D), np.float32); y_sample = np.zeros((128, 4, D), np.float32)
    nkp = np.zeros((2, 2, 128, 4, 64), np.float32); nvp = np.zeros_like(nkp); npp = np.zeros((2, 2, 15, 1024), np.float32)
    nks = np.zeros((2, 128, 128, 4, 64), np.float32); nvs = np.zeros_like(nks); nps = np.zeros((2, 128, 15, 1024), np.float32)
    for c in range(8):
        bq, j = c // 4, c % 4
        r = R[c]
        y_prompt[bq, j * 1024:(j + 1) * 1024] = r["y"][0:1024]
        y_sample[c * 16:(c + 1) * 16] = r["y"][1024:1088].reshape(16, 4, D)
        if j == 3:
            nkp[:, bq] = r["nkp"].reshape(2, 128, 4, 64); nvp[:, bq] = r["nvp"].reshape(2, 128, 4, 64); npp[:, bq] = r["npp"]
        nks[:, c * 16:(c + 1) * 16] = r["nks"].reshape(2, 16, 128, 4, 64); nvs[:, c * 16:(c + 1) * 16] = r["nvs"].reshape(2, 16, 128, 4, 64)
        nps[:, c * 16:(c + 1) * 16] = r["nps"]
    return (y_prompt, y_sample, nkp, nvp, npp, nks, nvs, nps)
```

```python
import numpy as np
from contextlib import ExitStack
import concourse.bass as bass
import concourse.mybir as mybir
from concourse.bass_utils import run_bass_kernel_spmd

F32 = mybir.dt.float32
BF16 = mybir.dt.bfloat16
AF = mybir.ActivationFunctionType
ALU = mybir.AluOpType
AX = mybir.AxisListType

D = 2048; NCH = 16; FF = 5632; NF = 44; NT = 1344
TILES = [(0, 448), (448, 896), (896, 1344)]
MTILES = [(0, 384), (384, 768), (768, 1152), (1152, 1344)]


def tiles_from(c0):
    n = NT - c0
    a = ((n + 2) // 3 + 1) // 2 * 2
    return [(c0, c0 + a), (c0 + a, c0 + 2 * a), (c0 + 2 * a, NT)]
SC = 1280
SCALE = 0.125
EPS = 1e-5
POOLW = (2, 4, 8, 16)
import os
STOP_AFTER = int(os.environ.get('KSTOP', '6'))
KSMALL = int(os.environ.get('KSMALL', '0'))
KFLAGS = os.environ.get('KFLAGS', '')
MSTOP = int(os.environ.get('MSTOP', '5'))


def units(a, b):
    return range(a // 64, (b + 63) // 64)


def K(name, cs, a, b):
    return [(name, c, u) for c in cs for u in units(a, b)]


class Prog:
    def __init__(s, nc, es):
        s.nc = nc; s.es = es; s.ops = []
        s.E = {'pe': nc.tensor, 'act': nc.scalar, 'dve': nc.vector, 'pool': nc.gpsimd, 'sp': nc.sync}
        s.out_sems = set()

    def op(s, eng, fn, R=(), W=()):
        s.ops.append(('c', eng, fn, list(R), list(W), None))

    def dma(s, q, out, in_, sem, R=(), W=(), is_out=False):
        s.ops.append(('d', q, (out, in_), list(R), list(W), sem))
        if is_out:
            s.out_sems.add(sem)

    def emit(s):
        nc = s.nc; ops = s.ops; n = len(ops)
        lastw = {}; readers = {}
        deps = [None] * n
        for i, (kind, eng, fn, R, W, sem) in enumerate(ops):
            d = set()
            for k in R:
                j = lastw.get(k)
                if j is not None: d.add(j)
            for k in W:
                j = lastw.get(k)
                if j is not None: d.add(j)
                rd = readers.get(k)
                if rd:
                    d.update(rd.values())
            d.discard(i)
            deps[i] = d
            rk = eng if kind == 'c' else ('dma', i)
            for k in W:
                lastw[k] = i; readers[k] = {}
            for k in R:
                readers.setdefault(k, {})[rk] = i
        seqno = [0] * n; ectr = {}
        for i in range(n):
            if ops[i][0] == 'c':
                ectr[ops[i][1]] = ectr.get(ops[i][1], 0) + 1
                seqno[i] = ectr[ops[i][1]]
        NEAR = 6

        def same_skip(i, j):
            if not (ops[i][0] == 'c' and ops[j][0] == 'c' and ops[i][1] == ops[j][1]):
                return False
            return ops[i][1] == 'pe' or (seqno[i] - seqno[j]) > NEAR
        needed = [False] * n
        for i in range(n):
            for j in deps[i]:
                if ops[j][0] == 'c' and not same_skip(i, j):
                    needed[j] = True
        cnt = {}; idx = [0] * n; dcount_before = [None] * n
        dsem_cnt = {}
        for i, (kind, eng, fn, R, W, sem) in enumerate(ops):
            if kind == 'c':
                if needed[i]:
                    cnt[eng] = cnt.get(eng, 0) + 1
                    idx[i] = cnt[eng]
            else:
                dsem_cnt[sem] = dsem_cnt.get(sem, 0) + 16
                idx[i] = dsem_cnt[sem]
        csem = {e: s.es.enter_context(nc.semaphore('c_' + e)) for e in ['pe', 'act', 'dve', 'pool']}
        dsem = {k: s.es.enter_context(nc.semaphore('d_' + str(k))) for k in dsem_cnt}
        seen = {e: {} for e in s.E}
        run_d = {k: 0 for k in dsem_cnt}
        nwaits = 0
        for i, (kind, eng, fn, R, W, sem) in enumerate(ops):
            Eng = s.E[eng]
            waits = {}
            for j in deps[i]:
                kj, ej = ops[j][0], ops[j][1]
                if kj == 'd':
                    key = ('d', ops[j][5]); val = run_d[ops[j][5]]
                elif same_skip(i, j):
                    continue
                else:
                    key = ('c', ej); val = idx[j]
                if val > waits.get(key, 0):
                    waits[key] = val
            for key, val in waits.items():
                if seen[eng].get(key, 0) >= val:
                    continue
                seen[eng][key] = val
                Eng.wait_ge(dsem[key[1]] if key[0] == 'd' else csem[key[1]], val)
                nwaits += 1
            if kind == 'c':
                ins = fn(Eng)
                if needed[i]:
                    ins.then_inc(csem[eng], 1)
            else:
                out, in_ = fn
                with nc.allow_non_contiguous_dma(reason="layout"):
                    Eng.dma_start(out=out, in_=in_).then_inc(dsem[sem], 16)
                run_d[sem] += 16
        for sem in sorted(s.out_sems, key=str):
            nc.sync.wait_ge(dsem[sem], dsem_cnt[sem])
        return n, nwaits


def build():
    nc = bass.Bass("TRN2", target_bir_lowering=False)
    es = ExitStack()
    P = Prog(nc, es)

    def din(name, shape):
        return nc.dram_tensor(name, list(shape), F32, kind="ExternalInput").ap()

    def dout(name, shape):
        return nc.dram_tensor(name, list(shape), F32, kind="ExternalOutput").ap()

    xin = din("xin", [NT, D])
    if KSMALL:
        Wg = Wu = Wd = [None, None]; w_in = w_out = None
        if KSMALL == 2:
            w_in = din("w_in", [2, D, 2560]); w_out = din("w_out", [2, D, D])
    else:
        Wg = [din("ffn1_gate", [2, D, FF]), din("ffn2_gate", [2, D, FF])]
        Wu = [din("ffn1_up", [2, D, FF]), din("ffn2_up", [2, D, FF])]
        Wd = [din("ffn1_down", [2, FF, D]), din("ffn2_down", [2, FF, D])]
        w_in = din("w_in", [2, D, 2560]); w_out = din("w_out", [2, D, D])
    ck = din("ck", [2, 16, 128, 256]); cv = din("cv", [2, 16, 128, 256]); spool = din("spool", [2, 16, 15, 1024])
    norms = din("norms", [7, D])
    pool_w = din("pool_w", [2, 4, 256, 256]); pscale_d = din("pool_scale", [2, 1024]); sinks_d = din("attn_sinks", [2, 16])
    identf_d = din("identf", [128, 128]); masks_d = din("masks", [128, 512]); smask_d = din("smask", [128, 1152])
    sel_d = din("sel", [128, 128]); invc_d = din("invc", [128, 64])
    y = dout("y", [1088, D]); nkp = dout("nkp", [2, 128, 256]); nvp = dout("nvp", [2, 128, 256]); npp = dout("npp", [2, 15, 1024])
    nks = dout("nks", [2, 16, 128, 256]); nvs = dout("nvs", [2, 16, 128, 256]); nps = dout("nps", [2, 16, 15, 1024])

    def sb(name, shape, dt):
        return es.enter_context(nc.sbuf_tensor(name, list(shape), dt))

    R1 = sb("R1", [128, NCH * NT], F32)
    R2 = sb("R2", [128, 21504], BF16)
    R3 = sb("R3", [128, 21504], BF16)
    R4 = sb("R4", [128, 8192], BF16)
    identf = sb("identf_s", [128, 128], F32); identb = sb("identb", [128, 128], BF16); onesb = sb("onesb", [128, 128], BF16); nhl = sb("nhl", [128, 896], BF16)
    gam = sb("gam", [128, 7 * 16], F32); pscale = sb("pscale", [128, 16], F32)
    nsink = sb("nsink", [128, 32], F32); nsrow = sb("nsrow", [128, 8], F32)
    masks = sb("masks_s", [128, 512], BF16); smask = sb("smask_s", [128, 1152], BF16); sel = sb("sel_s", [128, 128], BF16)
    invc = sb("invc_s", [128, 64], F32); epsb = sb("epsb", [128, 1], F32)
    ntmp = sb("ntmp", [128, 5 * 448], F32)
    stmp = sb("stmp", [128, 2 * 448], F32)
    PSA = es.enter_context(nc.psum_tensor("PSA", [128, 2048], F32))
    PSB = es.enter_context(nc.psum_tensor("PSB", [128, 2048], F32))

    def bank(b, n=512, off=0):
        t = PSA if b < 4 else PSB
        return t[:, (b % 4) * 512 + off:(b % 4) * 512 + off + n]

    def bankb(b):
        return bank(b).bitcast(BF16)

    x3 = R1[:, :].rearrange("p (c n) -> p c n", c=NCH)

    def r2(off, nbytes, dt=BF16):
        v = R2[:, off // 2:(off + nbytes) // 2]
        return v.bitcast(F32) if dt == F32 else v

    def r3(off, nbytes, dt=BF16):
        v = R3[:, off // 2:(off + nbytes) // 2]
        return v.bitcast(F32) if dt == F32 else v

    xn3 = R2[:, :].rearrange("p (c n) -> p c n", c=NCH)
    mix3 = R3[:, :].rearrange("p (c n) -> p c n", c=NCH)
    Hb = [r3(i * 10752, 10752).rearrange("p (f n) -> p f n", f=4) for i in range(2)]
    Wdb = [r3(21504 + i * 8192, 8192).rearrange("p (f n) -> p f n", f=4) for i in range(2)]
    stage = [r3(i * 8192, 8192, F32) for i in range(2)]
    wslot = [R4[:, i * 2048:(i + 1) * 2048].rearrange("p (k n) -> p k n", k=16) for i in range(4)]
    xnt3 = r2(0, 14336).rearrange("p (c n) -> p c n", c=NCH)
    kT3 = r2(14336, 10752).rearrange("p (g n) -> p g n", g=4)
    vtok = r2(25088, 5632).rearrange("p (b n) -> p b n", b=11)
    TM = 30720

    nsq = [ntmp[:, i * 448:(i + 1) * 448] for i in range(2)]
    nacc = ntmp[:, 896:1344]; nrs = ntmp[:, 1344:1792]; nrstd = ntmp[:, 1792:2240]
    sil = [stmp[:, i * 448:(i + 1) * 448] for i in range(2)]

    wctr = [0]

    def next_wslot():
        i = wctr[0] % 4; wctr[0] += 1
        return i

    P.dma('sp', identf[:, :], identf_d[:, :], 'cst', W=['identf'])
    P.dma('pool', masks[:, :], masks_d[:, :], 'cstp', W=['masks'])
    P.dma('pool', smask[:, :], smask_d[:, :], 'cstp', W=['smask'])
    P.dma('pool', sel[:, :], sel_d[:, :], 'cstp', W=['sel'])
    P.dma('pool', identb[:, :], identf_d[:, :], 'cstp', W=['identb'])
    P.dma('sp', invc[:, :], invc_d[:, :], 'cst', W=['invc'])
    P.dma('sp', stage[0][0:112, 0:128], norms.rearrange("j (c p) -> (j c) p", p=128), 'cst', W=[('stage', 0)])
    P.dma('sp', stage[0][0:16, 128:256], pscale_d.rearrange("l (c p) -> (l c) p", p=128), 'cst', W=[('stage', 0)])
    if 'nosink' not in KFLAGS:
        P.dma('sp', nsink[:, :], sinks_d.rearrange("l h -> (l h)").partition_broadcast(128), 'cst', W=['nsink'])
    sflat = sinks_d.rearrange("l h -> (l h)")
    for hp in range(2):
        for hh in range(2):
            src = sflat.rearrange("(q h) -> h q", h=4)[2 * hh + hp].partition_broadcast(32)
            if 'nosink' not in KFLAGS:
                P.dma('sp', nsrow[hp * 64 + hh * 32:hp * 64 + hh * 32 + 32, :], src, 'cst', W=['nsrow'])
    dummy = sb("dummy_bar", [128, 8], F32)

    def barrier(R, W):
        P.op('dve', lambda e: e.memset(dummy[:, 0:1], 0.0), R=R, W=W)

    ALLXN = K('xn', range(NCH), 0, NT)
    ALLMIX = K('mix', range(NCH), 0, NT)
    P.op('dve', lambda e: e.memset(onesb[:, :], 1.0), W=['onesb'])
    P.op('pe', lambda e: e.transpose(bank(0)[:, 0:112], stage[0][0:112, 0:128], identf[0:112, 0:112]), R=[('stage', 0), 'identf'], W=[('ps', 0)])
    P.op('pe', lambda e: e.transpose(bank(0)[:, 128:144], stage[0][0:16, 128:256], identf[0:16, 0:16]), R=[('stage', 0), 'identf'], W=[('ps', 0)])
    P.op('dve', lambda e: e.tensor_copy(out=gam[:, :], in_=bank(0)[:, 0:112]), R=[('ps', 0)], W=['gam'])
    P.op('dve', lambda e: e.tensor_copy(out=pscale[:, :], in_=bank(0)[:, 128:144]), R=[('ps', 0)], W=['pscale'])
    P.op('dve', lambda e: e.memset(epsb[:, :], EPS), W=['epsb'])
    P.op('dve', lambda e: e.tensor_scalar(out=nsink[:, :], in0=nsink[:, :], scalar1=-1.0, scalar2=None, op0=ALU.mult), R=['nsink'], W=['nsink'])
    P.op('dve', lambda e: e.tensor_scalar(out=nsrow[:, :], in0=nsrow[:, :], scalar1=-1.0, scalar2=None, op0=ALU.mult), R=['nsrow'], W=['nsrow'])

    evq = [0]

    def evac(out, in_, R, W):
        evq[0] += 1
        if evq[0] % 2:
            P.op('act', lambda e: e.activation(out=out, in_=in_, func=AF.Copy), R=R, W=W)
        else:
            P.op('dve', lambda e: e.tensor_copy(out=out, in_=in_), R=R, W=W)

    pb = 0
    for tb in range(11):
        rows = 128 if tb < 10 else 64
        st = stage[tb % 2]
        P.dma('sp', st[0:rows, :], xin[tb * 128:tb * 128 + rows, :], 'stg%d' % (tb % 2), W=[('stage', tb % 2)])
        for cg in range(4):
            b = pb % 4; pb += 1
            for i in range(4):
                c = cg * 4 + i
                P.op('pe', lambda e, b=b, i=i, c=c, st=st, rows=rows: e.transpose(bank(b)[:, i * 128:i * 128 + rows], st[0:rows, c * 128:(c + 1) * 128], identf[0:rows, 0:rows]),
                     R=[('stage', tb % 2), 'identf'], W=[('ps', b)])
            evac(x3[:, cg * 4:cg * 4 + 4, tb * 128:tb * 128 + rows], bank(b).rearrange("p (i n) -> p i n", i=4)[:, :, 0:rows],
                 R=[('ps', b)], W=K('x', range(cg * 4, cg * 4 + 4), tb * 128, tb * 128 + rows))

    barrier([('stage', 0), ('stage', 1)], ['r3go'])

    def rmsnorm(gi, a, b, out3, okey, ocol0):
        n = b - a
        for c in range(NCH):
            sq = nsq[c % 2]
            P.op('act', lambda e, c=c, sq=sq: e.activation(out=sq[:, 0:n], in_=x3[:, c, a:b], func=AF.Square),
                 R=K('x', [c], a, b), W=[('nsq', c % 2)])
            if c == 0:
                P.op('dve', lambda e, sq=sq: e.tensor_copy(out=nacc[:, 0:n], in_=sq[:, 0:n]), R=[('nsq', 0)], W=['nacc'])
            else:
                P.op('dve', lambda e, sq=sq: e.tensor_tensor(out=nacc[:, 0:n], in0=nacc[:, 0:n], in1=sq[:, 0:n], op=ALU.add),
                     R=[('nsq', c % 2), 'nacc'], W=['nacc'])
        hi = nhl[:, 0:n]; lo = nhl[:, 448:448 + n]
        P.op('dve', lambda e: e.tensor_copy(out=hi, in_=nacc[:, 0:n]), R=['nacc'], W=['nhi'])
        P.op('dve', lambda e: e.tensor_tensor(out=lo, in0=nacc[:, 0:n], in1=hi, op=ALU.subtract), R=['nacc', 'nhi'], W=['nlo'])
        P.op('pe', lambda e: e.matmul(bank(6)[:, 0:n], lhsT=onesb[:, :], rhs=hi, start=True, stop=False), R=['onesb', 'nhi'], W=[('ps', 6)])
        P.op('pe', lambda e: e.matmul(bank(6)[:, 0:n], lhsT=onesb[:, :], rhs=lo, start=False, stop=True), R=['onesb', 'nlo'], W=[('ps', 6)])
        P.op('act', lambda e: e.activation(out=nrs[:, 0:n], in_=bank(6)[:, 0:n], func=AF.Sqrt, bias=epsb[:, 0:1], scale=1.0 / D),
             R=[('ps', 6), 'epsb'], W=['nrs'])
        P.op('dve', lambda e: e.reciprocal(out=nrstd[:, 0:n], in_=nrs[:, 0:n]), R=['nrs'], W=['nrstd'])
        for c in range(NCH):
            P.op('dve', lambda e, c=c: e.scalar_tensor_tensor(out=out3[:, c, a - ocol0:b - ocol0], in0=x3[:, c, a:b], scalar=gam[:, gi * 16 + c:gi * 16 + c + 1],
                                                               in1=nrstd[:, 0:n], op0=ALU.mult, op1=ALU.mult),
                 R=K('x', [c], a, b) + ['nrstd', 'gam'], W=K(okey, [c], a, b))

    def ffn(fi_, l, gi, c0):
        TILES = tiles_from(c0)
        wg, wu, wd = Wg[fi_][l], Wu[fi_][l], Wd[fi_][l]
        wgv = wg.rearrange("(k p) n -> p k n", p=128); wuv = wu.rearrange("(k p) n -> p k n", p=128)
        for (a, b) in TILES:
            rmsnorm(gi, a, b, xn3, 'xn', 0)
        dcnt = [0]

        def GU(gr):
            for fi in range(4):
                f = gr * 4 + fi
                sg = next_wslot(); su = next_wslot()
                P.dma('pool', wslot[sg], wgv[:, :, f * 128:(f + 1) * 128], 'w%d' % sg, W=[('w', sg)])
                P.dma('pool', wslot[su], wuv[:, :, f * 128:(f + 1) * 128], 'w%d' % su, W=[('w', su)])
                if gr >= 1:
                    DN_dma(gr - 1, fi)
                for t, (a, b) in enumerate(TILES):
                    n = b - a
                    for (s_, bk) in ((sg, 2 * t), (su, 2 * t + 1)):
                        for k in range(NCH):
                            P.op('pe', lambda e, s_=s_, bk=bk, k=k, a=a, b=b, n=n: e.matmul(bank(bk)[:, 0:n], lhsT=wslot[s_][:, k, :], rhs=xn3[:, k, a:b], start=(k == 0), stop=(k == 15)),
                                 R=[('w', s_)] + K('xn', [k], a, b), W=[('ps', bk)])
                    sl = sil[t % 2]
                    P.op('act', lambda e, t=t, n=n, sl=sl: e.activation(out=sl[:, 0:n], in_=bank(2 * t)[:, 0:n], func=AF.Silu),
                         R=[('ps', 2 * t)], W=[('sil', t % 2)])
                    P.op('dve', lambda e, t=t, n=n, sl=sl, fi=fi, a=a, b=b, gr=gr: e.tensor_tensor(out=Hb[gr % 2][:, fi, a:b], in0=sl[:, 0:n], in1=bank(2 * t + 1)[:, 0:n], op=ALU.mult),
                         R=[('sil', t % 2), ('ps', 2 * t + 1)], W=K(('H', gr % 2), [fi], a, b))

        def DN_dma(gr, q):
            wdv = wd[gr * 512:(gr + 1) * 512, :].rearrange("(f p) n -> p f n", p=128)
            hm, hq = q // 2, q % 2
            P.dma('pool', Wdb[hm][:, :, hq * 512:(hq + 1) * 512], wdv[:, :, q * 512:(q + 1) * 512], 'wd%d' % q, R=['r3go'], W=[('wd', q)])

        def DN(gr):
            for m in range(NCH):
                hm = m // 8
                for t, (a, b) in enumerate(TILES):
                    n = b - a
                    bk = (6, 7, 4, 5)[dcnt[0] % 4]; dcnt[0] += 1
                    for fi in range(4):
                        P.op('pe', lambda e, bk=bk, fi=fi, m=m, hm=hm, a=a, b=b, n=n, gr=gr: e.matmul(bank(bk)[:, 0:n], lhsT=Wdb[hm][:, fi, (m % 8) * 128:(m % 8 + 1) * 128], rhs=Hb[gr % 2][:, fi, a:b], start=(fi == 0), stop=(fi == 3)),
                             R=[('wd', m // 4)] + K(('H', gr % 2), [fi], a, b), W=[('ps', bk)])
                    P.op('dve', lambda e, bk=bk, m=m, a=a, b=b, n=n: e.scalar_tensor_tensor(out=x3[:, m, a:b], in0=bank(bk)[:, 0:n], scalar=0.5, in1=x3[:, m, a:b], op0=ALU.mult, op1=ALU.add),
                         R=[('ps', bk)] + K('x', [m], a, b), W=K('x', [m], a, b))

        NG = NF // 4
        GU(0)
        for gr in range(1, NG):
            GU(gr)
            DN(gr - 1)
        for q in range(4):
            DN_dma(NG - 1, q)
        DN(NG - 1)

    def mixer(l):
        gi = 3 * l + 1
        winv = w_in[l].rearrange("(k p) n -> p k n", p=128)
        otmp = [r2(TM + i * 1024, 1024, F32) for i in range(2)]
        oc = [0]
        pbk = [0]

        def out_tok(psrc_fn, ncols, dst_fn):
            i = oc[0] % 2; oc[0] += 1
            ot = otmp[i]
            return i, ot

        barrier(ALLXN, ['mixgo'])
        MT = TILES if l == 0 else [(128, 512), (512, 896), (896, 1344)]
        for t, (a, b) in enumerate(MT):
            n = b - a
            rmsnorm(gi, a, b, xnt3, 'xnt', a)
            for cb in range(16):
                s_ = next_wslot()
                P.dma('pool', wslot[s_], winv[:, :, cb * 128:(cb + 1) * 128], 'w%d' % s_, W=[('w', s_)])
                bk = pbk[0] % 6; pbk[0] += 1
                for k in range(NCH):
                    P.op('pe', lambda e, s_=s_, bk=bk, k=k, n=n: e.matmul(bank(bk)[:, 0:n], lhsT=wslot[s_][:, k, :], rhs=xnt3[:, k, 0:n], start=(k == 0), stop=(k == 15)),
                         R=[('w', s_)] + K('xnt', [k], a, b), W=[('ps', bk)])
                evac(mix3[:, cb, a:b], bank(bk)[:, 0:n], R=[('ps', bk)], W=K('mix', [cb], a, b))
                if b == NT and cb < 8 and 'noutok' not in KFLAGS:
                    for (ca, cbb, rows, kind) in ((1152, 1280, 128, 'p'), (1280, 1344, 64, 's')):
                        bk2 = 6 + oc[0] % 2; i = oc[0] % 2; oc[0] += 1
                        for k in range(NCH):
                            P.op('pe', lambda e, s_=s_, bk2=bk2, k=k, ca=ca, cbb=cbb, rows=rows, a=a: e.matmul(bank(bk2)[0:rows, 0:128], lhsT=xnt3[:, k, ca - a:cbb - a], rhs=wslot[s_][:, k, :], start=(k == 0), stop=(k == 15)),
                                 R=[('w', s_)] + K('xnt', [k], ca, cbb), W=[('ps', bk2)])
                        ot = otmp[i]
                        P.op('dve', lambda e, bk2=bk2, rows=rows, ot=ot: e.tensor_copy(out=ot[0:rows, 0:128], in_=bank(bk2)[0:rows, 0:128]), R=[('ps', bk2)], W=[('otmp', i)])
                        if kind == 'p':
                            P.dma('sp', npp[l, :, cb * 128:(cb + 1) * 128], ot[113:128, 0:128], 'o%d' % i, R=[('otmp', i)], is_out=True)
                        else:
                            P.dma('sp', nps[l, :, 11:15, cb * 128:(cb + 1) * 128], ot[0:64, 0:128], 'o%d' % i, R=[('otmp', i)], is_out=True)
            kslots = []
            for g in range(4):
                s_ = next_wslot(); kslots.append(s_)
                P.dma('pool', wslot[s_][:, :, 0:64], winv[:, :, 2048 + g * 64:2048 + (g + 1) * 64], 'w%d' % s_, W=[('w', s_)])
                P.op('act', lambda e, s_=s_: e.activation(out=wslot[s_][:, :, 64:128], in_=wslot[s_][:, :, 0:64], func=AF.Copy), R=[('w', s_)], W=[('w', s_)])
                bk = pbk[0] % 6; pbk[0] += 1
                for k in range(NCH):
                    P.op('pe', lambda e, s_=s_, bk=bk, k=k, n=n: e.matmul(bank(bk)[:, 0:n], lhsT=wslot[s_][:, k, :], rhs=xnt3[:, k, 0:n], start=(k == 0), stop=(k == 15)),
                         R=[('w', s_)] + K('xnt', [k], a, b), W=[('ps', bk)])
                evac(kT3[:, g, a:b], bank(bk)[:, 0:n], R=[('ps', bk)], W=K('kT', [g], a, b))
            if b == NT and 'noktok' not in KFLAGS:
                for (ca, cbb, rows, kind) in ((1152, 1280, 128, 'p'), (1280, 1344, 64, 's')):
                    bk2 = 6 + oc[0] % 2; i = oc[0] % 2; oc[0] += 1
                    for g in range(4):
                        for k in range(NCH):
                            P.op('pe', lambda e, g=g, bk2=bk2, k=k, ca=ca, cbb=cbb, rows=rows, a=a, kslots=kslots: e.matmul(bank(bk2)[0:rows, g * 64:(g + 1) * 64], lhsT=xnt3[:, k, ca - a:cbb - a], rhs=wslot[kslots[g]][:, k, 0:64], start=(k == 0), stop=(k == 15)),
                                 R=[('w', kslots[g])] + K('xnt', [k], ca, cbb), W=[('ps', bk2)])
                    ot = otmp[i]
                    P.op('dve', lambda e, bk2=bk2, rows=rows, ot=ot: e.tensor_copy(out=ot[0:rows, :], in_=bank(bk2)[0:rows, 0:256]), R=[('ps', bk2)], W=[('otmp', i)])
                    if kind == 'p':
                        P.dma('sp', nkp[l], ot[:, :], 'o%d' % i, R=[('otmp', i)], is_out=True)
                    else:
                        P.dma('sp', nks[l, :, 124:128, :], ot[0:64, :], 'o%d' % i, R=[('otmp', i)], is_out=True)
            sv = [next_wslot(), next_wslot()]
            for j in range(2):
                P.dma('pool', wslot[sv[j]], winv[:, :, 2304 + j * 128:2304 + (j + 1) * 128], 'w%d' % sv[j], W=[('w', sv[j])])
            c0 = a if 'nov' not in KFLAGS else b
            while c0 < b:
                blk = c0 // 128
                c1 = min(b, (blk + 1) * 128)
                rows = c1 - c0; po = c0 - blk * 128
                bk = pbk[0] % 6; pbk[0] += 1
                for j in range(2):
                    for k in range(NCH):
                        P.op('pe', lambda e, j=j, bk=bk, k=k, c0=c0, c1=c1, rows=rows, po=po, a=a, sv=sv: e.matmul(bank(bk)[po:po + rows, j * 128:(j + 1) * 128], lhsT=xnt3[:, k, c0 - a:c1 - a], rhs=wslot[sv[j]][:, k, :], start=(k == 0), stop=(k == 15)),
                             R=[('w', sv[j])] + K('xnt', [k], c0, c1), W=[('ps', bk)])
                if blk >= 9:
                    P.op('dve', lambda e, bk=bk, rows=rows, po=po, blk=blk: e.tensor_copy(out=vtok[po:po + rows, blk, :], in_=bank(bk)[po:po + rows, 0:256]), R=[('ps', bk)], W=K('vtok', [blk], c0, c1))
                else:
                    evac(vtok[po:po + rows, blk, :], bank(bk)[po:po + rows, 0:256], R=[('ps', bk)], W=K('vtok', [blk], c0, c1))
                if blk >= 9 and 'novout' not in KFLAGS and not (blk == 10 and 'novs' in KFLAGS) and not (blk == 9 and 'novp' in KFLAGS):
                    i = oc[0] % 2; oc[0] += 1
                    ot = otmp[i]
                    P.op('dve', lambda e, bk=bk, rows=rows, po=po, ot=ot: e.tensor_copy(out=ot[po:po + rows, :], in_=bank(bk)[po:po + rows, 0:256]), R=[('ps', bk)], W=[('otmp', i)])
                    if blk == 9:
                        P.dma('sp', nvp[l], ot[:, :], 'o%d' % i, R=[('otmp', i)], is_out=True)
                    else:
                        P.dma('sp', nvs[l, :, 124:128, :], ot[0:64, :], 'o%d' % i, R=[('otmp', i)], is_out=True)
                c0 = c1
        if 'nopass' in KFLAGS:
            return
        P.dma('sp', nks[l, :, 0:124, :], ck[l, :, 4:128, :], 'oc', is_out=True)
        P.dma('sp', nvs[l, :, 0:124, :], cv[l, :, 4:128, :], 'oc', is_out=True)
        P.dma('sp', nps[l, :, 0:11, :], spool[l, :, 4:15, :], 'oc', is_out=True)

        barrier(K('xnt', range(NCH), 0, NT), ['m3go'])
        barrier([], ['m2go', ('w', 0), ('w', 1), ('w', 2), ('w', 3)])
        if MSTOP < 2:
            return
        def r4v(off, nbytes):
            return R4[:, off // 2:(off + nbytes) // 2].bitcast(F32)
        WSK = [('w', 0), ('w', 1), ('w', 2), ('w', 3)]
        def m2_gen():
            bufA = r4v(0, 5120); bufB = r4v(5120, 5120)
            hs = r4v(10240, 2432).rearrange("p (c s j) -> p c s j", c=2, s=16)
            dbuf = r2(TM + 2048, 5376).rearrange("p (c n) -> p c n", c=2)
            ststage = r2(TM + 2048 + 5376, 512, F32)
            pw = r2(TM + 2048 + 5376 + 512, 4096).rearrange("p (g k n) -> p g k n", g=4, k=2)
            P.dma('pool', pw, pool_w[l].rearrange("g (k p) n -> p g k n", p=128), 'pw', W=['pw'])
            for g in range(4):
                w = POOLW[g]
                for ci in range(2):
                    c = 2 * g + ci
                    u = mix3[:, c, 0:SC]
                    src = None
                    bufs = [bufA, bufB]
                    cur = u; sh = 1; bi = 0
                    for step in range(g + 1):
                        dst = bufs[bi]
                        P.op('pool', lambda e, dst=dst, cur=cur, sh=sh: e.tensor_tensor(out=dst[:, sh:SC], in0=cur[:, sh:SC], in1=cur[:, 0:SC - sh], op=ALU.add),
                             R=K('mix', [c], 0, SC) + [('pbuf', 1 - bi), 'm2go'], W=[('pbuf', bi)])
                        P.op('pool', lambda e, dst=dst, cur=cur, sh=sh: e.tensor_copy(out=dst[:, 0:sh], in_=cur[:, 0:sh]),
                             R=K('mix', [c], 0, SC) + [('pbuf', 1 - bi), 'm2go'], W=[('pbuf', bi)])
                        cur = dst; sh *= 2; bi = 1 - bi
                    fb = 1 - bi
                    P.op('dve', lambda e, cur=cur, u=u, ci=ci, w=w: e.scalar_tensor_tensor(out=dbuf[:, ci, 0:SC], in0=cur[:, 0:SC], scalar=1.0 / w, in1=u, op0=ALU.mult, op1=ALU.subtract),
                         R=[('pbuf', fb)] + K('mix', [c], 0, SC), W=K('dbuf', [ci], 0, SC))
                    P.op('dve', lambda e, cur=cur, g=g: e.tensor_tensor(out=ntmp[:, 64:80], in0=cur[:, 256:272], in1=invc[:, g * 16:(g + 1) * 16], op=ALU.mult),
                         R=[('pbuf', fb), 'invc', ('nsq', 0)], W=[('nsq', 0)])
                    P.op('dve', lambda e, u=u, ci=ci: e.tensor_tensor(out=dbuf[:, ci, 256:272], in0=ntmp[:, 64:80], in1=u[:, 256:272], op=ALU.subtract),
                         R=[('nsq', 0)] + K('mix', [c], 256, 272), W=K('dbuf', [ci], 256, 272))
                    for bt in range(2):
                        P.dma('sp', ststage[0:120, :], spool[l, bt * 8:(bt + 1) * 8, :, c * 128:(c + 1) * 128].rearrange("b r n -> (b r) n"), 'sst', W=['ststage'])
                        P.op('pe', lambda e: e.transpose(bank(6)[:, 0:120], ststage[0:120, :], identf[0:120, 0:120]), R=['ststage', 'identf'], W=[('ps', 6)])
                        P.op('dve', lambda e, ci=ci, bt=bt: e.tensor_copy(out=hs[:, ci, bt * 8:(bt + 1) * 8, 0:15], in_=bank(6)[:, 0:120].rearrange("p (s j) -> p s j", s=8)),
                             R=[('ps', 6)], W=[('hs', ci)])
                    P.op('dve', lambda e, ci=ci, c=c: e.tensor_copy(out=hs[:, ci, :, 15:19], in_=mix3[:, c, SC:NT].rearrange("p (s t) -> p s t", t=4)),
                         R=K('mix', [c], SC, NT), W=[('hs', ci)])
                    for t4 in range(4):
                        P.op('dve', lambda e, ci=ci, t4=t4, w=w: e.tensor_reduce(out=bufA[:, 1280 - 64 + t4 * 16:1280 - 64 + (t4 + 1) * 16] if False else ntmp[:, t4 * 16:(t4 + 1) * 16], in_=hs[:, ci, :, 16 + t4 - w:16 + t4], op=ALU.add, axis=AX.X),
                             R=[('hs', ci), ('nsq', 0)], W=['psum4', ('nsq', 0)])
                    P.op('dve', lambda e, ci=ci, c=c, w=w: e.scalar_tensor_tensor(out=dbuf[:, ci, SC:NT].rearrange("p (s t) -> p s t", t=4), in0=ntmp[:, 0:64].rearrange("p (t s) -> p s t", t=4), scalar=1.0 / w,
                                                                                 in1=mix3[:, c, SC:NT].rearrange("p (s t) -> p s t", t=4), op0=ALU.mult, op1=ALU.subtract),
                         R=['psum4'] + K('mix', [c], SC, NT), W=K('dbuf', [ci], SC, NT))
                    yield
                for mo in range(2):
                    c = 2 * g + mo
                    for t, (a, b) in enumerate(TILES):
                        n = b - a
                        bk = pbk[0] % 6; pbk[0] += 1
                        for ki in range(2):
                            P.op('pe', lambda e, bk=bk, ki=ki, mo=mo, g=g, a=a, b=b, n=n: e.matmul(bank(bk)[:, 0:n], lhsT=pw[:, g, ki, mo * 128:(mo + 1) * 128], rhs=dbuf[:, ki, a:b], start=(ki == 0), stop=(ki == 1)),
                                 R=['pw'] + K('dbuf', [ki], a, b), W=[('ps', bk)])
                        P.op('act', lambda e, bk=bk, c=c, a=a, b=b, n=n: e.activation(out=mix3[:, c, a:b], in_=bank(bk)[:, 0:n], func=AF.Copy, scale=pscale[:, l * 8 + c:l * 8 + c + 1]),
                             R=[('ps', bk), 'pscale'] + K('dbuf', [0, 1], a, b), W=K('mix', [c], a, b))
                yield

        if MSTOP < 3:
            for _ in m2_gen():
                pass
            return
        XB = 0
        pbuf = [r2(XB + i * 2048, 2048).rearrange("p (h n) -> p h n", h=4) for i in range(2)]
        ptsb = [r2(XB + 4096 + i * 2048, 2048).rearrange("p (j n) -> p j n", j=8) for i in range(2)]
        onb = [r2(XB + 8192 + i * 512, 512).rearrange("p (h n) -> p h n", h=4) for i in range(2)]
        sm = [r2(XB + 9216 + i * 128, 128, F32) for i in range(4)]
        def smv(j):
            smi = sm[j]
            return smi[:, 0:4], smi[:, 4:8], smi[:, 8:12], smi[:, 12:16], smi[:, 16:20], smi[:, 20:24]

        def stA(n_, blk, g):
            i = n_ % 2; j = n_ % 4
            Sb = (0, 1) if i == 0 else (2, 3)
            mk = masks[:, 0:256] if blk == 2 else masks[:, 256:512]
            S4 = (PSA[:, 0:1024] if i == 0 else PSA[:, 1024:2048]).rearrange("p (h n) -> p h n", h=4)
            for h in range(4):
                po = (h % 2) * 64
                reg = bank(Sb[h // 2])[:, (h % 2) * 256:(h % 2 + 1) * 256]
                P.op('pe', lambda e, reg=reg, mk=mk: e.matmul(reg, lhsT=identb[:, :], rhs=mk, start=True, stop=False),
                     R=['identb', 'masks'], W=[('ps', Sb[h // 2])])
                P.op('pe', lambda e, reg=reg, po=po, g=g, h=h, blk=blk: e.matmul(reg, lhsT=mix3[po:po + 64, 8 + 2 * g + h // 2, blk * 128:(blk + 1) * 128],
                                                                                  rhs=kT3[po:po + 64, g, (blk - 1) * 128:(blk + 1) * 128], start=False, stop=True),
                     R=K('mix', [8 + 2 * g + h // 2], blk * 128, (blk + 1) * 128) + K('kT', [g], (blk - 1) * 128, (blk + 1) * 128), W=[('ps', Sb[h // 2])])
            rmax, negm, rsum, dd, esv, rinv = smv(j)
            nsg = nsink[:, l * 16 + 4 * g:l * 16 + 4 * g + 4]
            P.op('dve', lambda e, S4=S4, rmax=rmax: e.reduce_max(out=rmax, in_=S4, axis=AX.X), R=[('ps', Sb[0]), ('ps', Sb[1])], W=[('sm', j, 0)])
            P.op('dve', lambda e, rmax=rmax, negm=negm, nsg=nsg: e.scalar_tensor_tensor(out=negm, in0=rmax, scalar=-SCALE, in1=nsg, op0=ALU.mult, op1=ALU.min), R=[('sm', j, 0), 'nsink'], W=[('sm', j, 1)])
            P.op('dve', lambda e, dd=dd, negm=negm, nsg=nsg: e.tensor_tensor(out=dd, in0=negm, in1=nsg, op=ALU.subtract), R=[('sm', j, 1), 'nsink'], W=[('sm', j, 3)])
            for h in range(4):
                P.op('act', lambda e, h=h, S4=S4, negm=negm, i=i: e.activation(out=pbuf[i][:, h, :], in_=S4[:, h, :], func=AF.Exp, bias=negm[:, h:h + 1], scale=SCALE),
                     R=[('ps', Sb[h // 2]), ('sm', j, 1)], W=[('p', i)])
            P.op('act', lambda e, dd=dd, esv=esv: e.activation(out=esv, in_=dd, func=AF.Exp), R=[('sm', j, 3)], W=[('sm', j, 4)])
            P.op('dve', lambda e, i=i, rsum=rsum: e.reduce_sum(out=rsum, in_=pbuf[i], axis=AX.X), R=[('p', i)], W=[('sm', j, 2)])
            P.op('dve', lambda e, rsum=rsum, esv=esv, rinv=rinv: e.tensor_tensor(out=rinv, in0=rsum, in1=esv, op=ALU.add), R=[('sm', j, 2), ('sm', j, 4)], W=[('sm', j, 5)])
            P.op('dve', lambda e, rinv=rinv: e.reciprocal(out=rinv, in_=rinv), R=[('sm', j, 5)], W=[('sm', j, 5)])

        def stB(n_, blk, g):
            i = n_ % 2
            ptb = 4 + i
            for h in range(4):
                for kb in range(2):
                    P.op('pe', lambda e, h=h, kb=kb, i=i, ptb=ptb: e.transpose(bankb(ptb)[:, (h * 2 + kb) * 128:(h * 2 + kb + 1) * 128], pbuf[i][:, h, kb * 128:(kb + 1) * 128], identb[:, :]),
                         R=[('p', i), 'identb'], W=[('ps', ptb)])
            P.op('act', lambda e, i=i, ptb=ptb: e.activation(out=ptsb[i], in_=bankb(ptb).rearrange("p (j n) -> p j n", j=8), func=AF.Copy), R=[('ps', ptb)], W=[('pt', i)])

        def stC(n_, blk, g):
            i = n_ % 2; j = n_ % 4
            rinv = smv(j)[5]
            ob = 6 + i
            for h in range(4):
                for kb in range(2):
                    P.op('pe', lambda e, h=h, kb=kb, i=i, ob=ob, g=g, blk=blk: e.matmul(bank(ob)[:, h * 64:(h + 1) * 64], lhsT=ptsb[i][:, h * 2 + kb, :], rhs=vtok[:, blk - 1 + kb, g * 64:(g + 1) * 64], start=(kb == 0), stop=(kb == 1)),
                         R=[('pt', i)] + K('vtok', [blk - 1 + kb], (blk - 1 + kb) * 128, (blk + kb) * 128), W=[('ps', ob)])
            P.op('dve', lambda e, i=i, ob=ob, rinv=rinv: e.tensor_tensor(out=onb[i], in0=bank(ob)[:, 0:256].rearrange("p (h n) -> p h n", h=4), in1=rinv.unsqueeze(2).broadcast_to([128, 4, 64]), op=ALU.mult),
                 R=[('ps', ob), ('sm', j, 5)], W=[('on', i)])

        def stD(n_, blk, g):
            i = n_ % 2
            ob = 6 + i
            otv = bank(ob)[:, 256:384].bitcast(BF16)
            for hh in range(2):
                P.op('pe', lambda e, hh=hh, i=i, otv=otv: e.transpose(otv[:, hh * 128:(hh + 1) * 128], onb[i][:, 2 * hh:2 * hh + 2, :].rearrange("p h n -> p (h n)"), identb[:, :]),
                     R=[('on', i), 'identb'], W=[('ps', ob)])
            P.op('act', lambda e, g=g, blk=blk, otv=otv: e.activation(out=mix3[:, 8 + 2 * g:8 + 2 * g + 2, blk * 128:(blk + 1) * 128], in_=otv.rearrange("p (h n) -> p h n", h=2), func=AF.Copy),
                 R=[('ps', ob)], W=K('mix', [8 + 2 * g, 8 + 2 * g + 1], blk * 128, (blk + 1) * 128))

        unitsl = [(blk, g) for blk in range(1 + l, 10) for g in range(4)]
        NU = len(unitsl)
        m2g = m2_gen()
        for k in range(NU + 3):
            for st, off in ((stA, 0), (stB, 1), (stC, 2), (stD, 3)):
                n_ = k - off
                if 0 <= n_ < NU:
                    st(n_, unitsl[n_][0], unitsl[n_][1])

        if MSTOP < 4:
            for _ in m2g:
                pass
            return
        kc = r2(XB, 2048).rearrange("p (s u d) -> p s u d", s=8, u=2)
        vc = r2(XB + 2048, 4096).rearrange("p (s n) -> p s n", s=8)
        kcT = r2(XB + 6144, 2048).rearrange("p (s n) -> p s n", s=8)
        ps_ = r2(XB + 8192, 2176)
        pts = r2(XB + 10368, 2048).rearrange("p (s n) -> p s n", s=8)
        ptn = r2(XB + 12416, 256)
        onA = r2(XB + 12672, 256); onB = r2(XB + 12928, 256)
        sms = r2(XB + 13184, 128, F32)
        qs = r2(XB + 13312, 128)
        M3K = [('on', 0), ('on', 1), ('p', 0), ('p', 1), ('pt', 0), ('pt', 1)] + [('sm', i_, j_) for i_ in range(4) for j_ in range(6)]
        barrier(M3K, ['m4go'])
        P.op('dve', lambda e: e.memset(onA[:, :], 0.0), W=['onA'])
        P.op('dve', lambda e: e.memset(onB[:, :], 0.0), W=['onB'])
        S = PSA[:, 0:1088]
        for bt in range(2):
            P.dma('pool', vc, cv[l, bt * 8:(bt + 1) * 8, :, :].rearrange("s k n -> k s n"), 'vc', R=['m4go'], W=['vc'])
            for g in range(4):
                for dup in range(2):
                    P.dma('pool', kc[:, :, dup, :], ck[l, bt * 8:(bt + 1) * 8, :, g * 64:(g + 1) * 64].rearrange("s k d -> k s d"), 'kc', R=['m4go'], W=['kc'])
                for s8 in range(8):
                    P.op('pe', lambda e, s8=s8: e.transpose(bankb(3)[:, s8 * 128:(s8 + 1) * 128], kc[:, s8, :, :].rearrange("p u d -> p (u d)"), identb[:, :]),
                         R=['kc', 'identb'], W=[('ps', 3)])
                P.op('act', lambda e: e.activation(out=kcT, in_=bankb(3).rearrange("p (s n) -> p s n", s=8), func=AF.Copy), R=[('ps', 3)], W=['kcT'])
                qcols = slice(SC + bt * 32, SC + bt * 32 + 32)
                P.op('dve', lambda e, g=g, qcols=qcols: e.tensor_copy(out=qs[:, :].rearrange("p (h n) -> p h n", h=2), in_=mix3[:, 8 + 2 * g:8 + 2 * g + 2, qcols]),
                     R=K('mix', [8 + 2 * g, 8 + 2 * g + 1], SC, NT), W=['qs'])
                for hp in range(2):
                    po = hp * 64
                    lh = qs[po:po + 64, :]
                    for s8 in range(9):
                        if s8 < 8:
                            reg = S[po:po + 64, s8 * 128:(s8 + 1) * 128]; mk = smask[:, s8 * 128:(s8 + 1) * 128]; rh = kcT[po:po + 64, s8, :]
                            RR = ['kcT', 'qs']
                        else:
                            reg = S[po:po + 64, 1024:1088]; mk = smask[:, 1024 + bt * 64:1088 + bt * 64]; rh = kT3[po:po + 64, g, SC:NT]
                            RR = K('kT', [g], SC, NT) + ['qs']
                        bkk = min(s8 // 4, 2)
                        P.op('pe', lambda e, reg=reg, mk=mk, po=po: e.matmul(reg, lhsT=identb[:, po:po + 64], rhs=mk, start=True, stop=False),
                             R=['identb', 'smask'], W=[('ps', bkk)])
                        P.op('pe', lambda e, reg=reg, lh=lh, rh=rh: e.matmul(reg, lhsT=lh, rhs=rh, start=False, stop=True),
                             R=RR, W=[('ps', bkk)])
                rmax = sms[:, 0:1]; negm = sms[:, 1:2]; rsum = sms[:, 2:3]; dd = sms[:, 3:4]; esv = sms[:, 4:5]; rinv = sms[:, 5:6]
                nsg = nsrow[:, l * 4 + g:l * 4 + g + 1]
                SK = [('ps', 0), ('ps', 1), ('ps', 2)]
                P.op('dve', lambda e, rmax=rmax: e.reduce_max(out=rmax, in_=S, axis=AX.X), R=SK, W=['sms0'])
                P.op('dve', lambda e, rmax=rmax, negm=negm: e.tensor_scalar(out=negm, in0=rmax, scalar1=-SCALE, scalar2=None, op0=ALU.mult), R=['sms0'], W=['sms1'])
                P.op('dve', lambda e, negm=negm, nsg=nsg: e.tensor_tensor(out=negm, in0=negm, in1=nsg, op=ALU.min), R=['sms1', 'nsrow'], W=['sms1'])
                P.op('act', lambda e, negm=negm: e.activation(out=ps_, in_=S, func=AF.Exp, bias=negm, scale=SCALE), R=SK + ['sms1'], W=['ps_'])
                P.op('dve', lambda e, rsum=rsum: e.reduce_sum(out=rsum, in_=ps_, axis=AX.X), R=['ps_'], W=['sms2'])
                P.op('dve', lambda e, dd=dd, negm=negm, nsg=nsg: e.tensor_tensor(out=dd, in0=negm, in1=nsg, op=ALU.subtract), R=['sms1', 'nsrow'], W=['sms3'])
                P.op('act', lambda e, dd=dd, esv=esv: e.activation(out=esv, in_=dd, func=AF.Exp), R=['sms3'], W=['sms4'])
                P.op('dve', lambda e, rsum=rsum, esv=esv, rinv=rinv: e.tensor_tensor(out=rinv, in0=rsum, in1=esv, op=ALU.add), R=['sms2', 'sms4'], W=['sms5'])
                P.op('dve', lambda e, rinv=rinv: e.reciprocal(out=rinv, in_=rinv), R=['sms5'], W=['sms5'])
                for s8 in range(8):
                    P.op('pe', lambda e, s8=s8: e.transpose(bankb(4)[:, s8 * 128:(s8 + 1) * 128], ps_[:, s8 * 128:(s8 + 1) * 128], identb[:, :]), R=['ps_', 'identb'], W=[('ps', 4)])
                P.op('pe', lambda e: e.transpose(bankb(5)[0:64, 0:128], ps_[:, 1024:1088], identb[:, :]), R=['ps_', 'identb'], W=[('ps', 5)])
                P.op('act', lambda e: e.activation(out=pts, in_=bankb(4).rearrange("p (s n) -> p s n", s=8), func=AF.Copy), R=[('ps', 4)], W=['pts'])
                P.op('dve', lambda e: e.tensor_copy(out=ptn[0:64, :], in_=bankb(5)[0:64, 0:128]), R=[('ps', 5)], W=['ptn'])
                for s8 in range(8):
                    P.op('pe', lambda e, s8=s8, g=g: e.matmul(bank(6)[:, 0:64], lhsT=pts[:, s8, :], rhs=vc[:, s8, g * 64:(g + 1) * 64], start=(s8 == 0), stop=False),
                         R=['pts', 'vc'], W=[('ps', 6)])
                P.op('pe', lambda e, g=g: e.matmul(bank(6)[:, 0:64], lhsT=ptn[0:64, :], rhs=vtok[0:64, 10, g * 64:(g + 1) * 64], start=False, stop=True),
                     R=['ptn'] + K('vtok', [10], SC, NT), W=[('ps', 6)])
                P.op('dve', lambda e, rinv=rinv: e.tensor_scalar(out=onA[:, 0:64], in0=bank(6)[:, 0:64], scalar1=rinv, scalar2=None, op0=ALU.mult), R=[('ps', 6), 'sms5'], W=['onA'])
                P.op('dve', lambda e, rinv=rinv: e.tensor_scalar(out=onB[:, 64:128], in0=bank(6)[:, 0:64], scalar1=rinv, scalar2=None, op0=ALU.mult), R=[('ps', 6), 'sms5'], W=['onB'])
                for hh in range(2):
                    P.op('pe', lambda e, hh=hh: e.matmul(bank(7)[:, hh * 32:(hh + 1) * 32], lhsT=onA[:, :], rhs=sel[:, (hh * 2 + 0) * 32:(hh * 2 + 1) * 32], start=True, stop=False),
                         R=['onA', 'sel'], W=[('ps', 7)])
                    P.op('pe', lambda e, hh=hh: e.matmul(bank(7)[:, hh * 32:(hh + 1) * 32], lhsT=onB[:, :], rhs=sel[:, (hh * 2 + 1) * 32:(hh * 2 + 2) * 32], start=False, stop=True),
                         R=['onB', 'sel'], W=[('ps', 7)])
                P.op('act', lambda e, g=g, qcols=qcols: e.activation(out=mix3[:, 8 + 2 * g:8 + 2 * g + 2, qcols], in_=bank(7)[:, 0:64].rearrange("p (h n) -> p h n", h=2), func=AF.Copy),
                     R=[('ps', 7)], W=K('mix', [8 + 2 * g, 8 + 2 * g + 1], SC, NT))
                next(m2g, None); next(m2g, None)
        for _ in m2g:
            pass
        barrier([('pbuf', 0), ('pbuf', 1), ('hs', 0), ('hs', 1)], [('w', 0), ('w', 1), ('w', 2), ('w', 3)])

        if MSTOP < 5:
            return
        wov = w_out[l].rearrange("(k p) n -> p k n", p=128)
        dc = 0
        for mb in range(NCH):
            s_ = next_wslot()
            P.dma('pool', wslot[s_], wov[:, :, mb * 128:(mb + 1) * 128], 'w%d' % s_, W=[('w', s_)])
            for t, (a, b) in enumerate(tiles_from(128 * (l + 1))):
                n = b - a
                bk = dc % 6; dc += 1
                for k in range(NCH):
                    P.op('pe', lambda e, s_=s_, bk=bk, k=k, a=a, b=b, n=n: e.matmul(bank(bk)[:, 0:n], lhsT=wslot[s_][:, k, :], rhs=mix3[:, k, a:b], start=(k == 0), stop=(k == 15)),
                         R=[('w', s_)] + K('mix', [k], a, b), W=[('ps', bk)])
                P.op('dve', lambda e, bk=bk, mb=mb, a=a, b=b, n=n: e.tensor_tensor(out=x3[:, mb, a:b], in0=bank(bk)[:, 0:n], in1=x3[:, mb, a:b], op=ALU.add),
                     R=[('ps', bk)] + K('x', [mb], a, b), W=K('x', [mb], a, b))

    MIXK = ['qs', 'kc', 'vc', 'kcT', 'ps_', 'pts', 'ptn', 'onA', 'onB', 'pw', 'ststage', ('otmp', 0), ('otmp', 1)] + K('kT', range(4), 0, NT) + K('vtok', range(11), 0, 128) + K('dbuf', range(2), 0, NT) + K('xnt', range(NCH), 0, NT)

    ph = 0
    for l in range(2):
        for fi_, fn_ in enumerate((lambda: ffn(0, l, 3 * l + 0, 128 * l), lambda: (mixer(l), barrier(MIXK + ALLMIX, ['r3go'])), lambda: ffn(1, l, 3 * l + 2, 128 * (l + 1)))):
            if ph < STOP_AFTER and not ('noffn' in KFLAGS and fi_ != 1):
                fn_()
            ph += 1

    for (a, b) in tiles_from(256):
        if 'nonorm' not in KFLAGS:
            rmsnorm(6, a, b, x3, 'x', 0)
    pb = 0
    for tb in (range(2, 11) if 'nofinal' not in KFLAGS else []):
        rows = 128 if tb < 10 else 64
        st = stage[tb % 2]
        for cg in range(4):
            b = pb % 4; pb += 1
            for i in range(4):
                c = cg * 4 + i
                P.op('pe', lambda e, b=b, i=i, c=c, rows=rows, tb=tb: e.transpose(bank(b)[0:rows, i * 128:(i + 1) * 128], x3[:, c, tb * 128:tb * 128 + rows], identf[:, :]),
                     R=K('x', [c], tb * 128, tb * 128 + rows) + ['identf'], W=[('ps', b)])
            evac(st[0:rows, cg * 512:(cg + 1) * 512], bank(b)[0:rows, :], R=[('ps', b)], W=[('stage', tb % 2)])
        P.dma('sp', y[(tb - 2) * 128:(tb - 2) * 128 + rows, :], st[0:rows, :], 'stg%d' % (tb % 2), R=[('stage', tb % 2)], is_out=True)

    if 'dbgmix' in KFLAGS:
        dbg = dout("dbg", [128, 10752]); dbgx = dout("dbgx", [128, NCH * NT])
        P.dma('sp', dbg, R3[:, :].bitcast(F32), 'dbg', R=ALLMIX + [('stage', 0), ('stage', 1)], is_out=True)
        P.dma('sp', dbgx, R1[:, :], 'dbg', R=K('x', range(NCH), 0, NT), is_out=True)
    nops, nwaits = P.emit()
    return nc, es, nops, nwaits


_CACHE = {}


def _consts():
    identf = np.eye(128, dtype=np.float32)
    i = np.arange(128)[:, None]; j = np.arange(256)[None, :]
    rel = i + 128 - j
    band = ((rel >= 0) & (rel < 128))
    NEG = -30000.0
    maskB = np.where(band, 0.0, NEG).astype(np.float32)
    maskA0 = np.where(band & (j >= 128), 0.0, NEG).astype(np.float32)
    sm = np.full((2, 128, 1088), NEG, np.float32)
    sel = np.zeros((128, 2, 2, 32), np.float32)
    for r in range(128):
        hp = r // 64; hh = (r % 64) // 32; ii = (r % 32) // 4; t = r % 4
        sel[r, hh, hp, ii * 4 + t] = 1.0
        for bt in range(2):
            sm[bt, r, ii * 128 + t + 1:ii * 128 + 128] = 0.0
            sq = bt * 8 + ii
            sm[bt, r, 1024 + sq * 4:1024 + sq * 4 + t + 1] = 0.0
    return identf, maskA0, maskB, sm, sel.reshape(128, 128)


def kernel(x_prompt, x_sample, cache_k, cache_v, state_pool, norm_ffn1, ffn1_gate, ffn1_up, ffn1_down, norm_mix, w_in,
           pool_w, pool_scale, attn_sinks, w_out, norm_ffn2, ffn2_gate, ffn2_up, ffn2_down, final_norm):
    f32 = lambda a: np.ascontiguousarray(np.asarray(a, dtype=np.float32))
    x_prompt = f32(x_prompt); x_sample = f32(x_sample); cache_k = f32(cache_k); cache_v = f32(cache_v); state_pool = f32(state_pool)
    if 'nc' not in _CACHE:
        _CACHE['nc'] = build()
    nc = _CACHE['nc'][0]
    identf, maskA0, maskB, sm, sel = _consts()
    norms = np.stack([f32(norm_ffn1)[0], f32(norm_mix)[0], f32(norm_ffn2)[0], f32(norm_ffn1)[1], f32(norm_mix)[1], f32(norm_ffn2)[1], f32(final_norm)], 0)
    shared = dict(ffn1_gate=f32(ffn1_gate), ffn1_up=f32(ffn1_up), ffn1_down=f32(ffn1_down), ffn2_gate=f32(ffn2_gate), ffn2_up=f32(ffn2_up),
                  ffn2_down=f32(ffn2_down), w_in=f32(w_in), w_out=f32(w_out)) if not KSMALL else ({} if KSMALL == 1 else dict(w_in=f32(w_in), w_out=f32(w_out)))
    shared.update(norms=np.ascontiguousarray(norms), pool_w=f32(pool_w),
                  pool_scale=f32(pool_scale), attn_sinks=f32(attn_sinks), identf=identf, sel=sel)
    xs = x_sample.reshape(128 * 4, D)
    ckr = cache_k.reshape(2, 128, 128, 256); cvr = cache_v.reshape(2, 128, 128, 256)
    in_maps = []
    for c in range(8):
        bq, j = c // 4, c % 4
        xin = np.zeros((NT, D), np.float32)
        if j > 0:
            xin[0:256] = x_prompt[bq, j * 1024 - 256:j * 1024]
        xin[256:1280] = x_prompt[bq, j * 1024:(j + 1) * 1024]
        xin[1280:1344] = xs[c * 64:(c + 1) * 64]
        masks = np.concatenate([maskA0 if j == 0 else maskB, maskB], axis=1)
        invc = np.zeros((128, 4, 16), np.float32)
        for g, w in enumerate(POOLW):
            if j == 0:
                invc[:, g, :] = (1.0 / np.minimum(w, np.arange(16) + 1.0))[None, :]
            else:
                invc[:, g, :] = 1.0 / w
        m = dict(shared)
        m.update(xin=xin, ck=np.ascontiguousarray(ckr[:, c * 16:(c + 1) * 16]), cv=np.ascontiguousarray(cvr[:, c * 16:(c + 1) * 16]),
                 spool=np.ascontiguousarray(state_pool[:, c * 16:(c + 1) * 16]), masks=np.ascontiguousarray(masks),
                 smask=np.ascontiguousarray(np.concatenate([sm[0], sm[1][:, 1024:1088]], axis=1)), invc=np.ascontiguousarray(invc.reshape(128, 64)))
        in_maps.append(m)
    res = run_bass_kernel_spmd(nc, in_maps, core_ids=list(range(8)))
    R = res.results
    _CACHE["res"] = R
    y_prompt = np.zeros((2, 4096, D), np.float32); y_sample = np.zeros((128, 4, D), np.float32)
    nkp = np.zeros((2, 2, 128, 4, 64), np.float32); nvp = np.zeros_like(nkp); npp = np.zeros((2, 2, 15, 1024), np.float32)
    nks = np.zeros((2, 128, 128, 4, 64), np.float32); nvs = np.zeros_like(nks); nps = np.zeros((2, 128, 15, 1024), np.float32)
    for c in range(8):
        bq, j = c // 4, c % 4
        r = R[c]
        y_prompt[bq, j * 1024:(j + 1) * 1024] = r["y"][0:1024]
        y_sample[c * 16:(c + 1) * 16] = r["y"][1024:1088].reshape(16, 4, D)
        if j == 3:
            nkp[:, bq] = r["nkp"].reshape(2, 128, 4, 64); nvp[:, bq] = r["nvp"].reshape(2, 128, 4, 64); npp[:, bq] = r["npp"]
        nks[:, c * 16:(c + 1) * 16] = r["nks"].reshape(2, 16, 128, 4, 64); nvs[:, c * 16:(c + 1) * 16] = r["nvs"].reshape(2, 16, 128, 4, 64)
        nps[:, c * 16:(c + 1) * 16] = r["nps"]
    return (y_prompt, y_sample, nkp, nvp, npp, nks, nvs, nps)
```

```python
import numpy as np
from contextlib import ExitStack
import concourse.bass as bass
import concourse.mybir as mybir
from concourse.bass_utils import run_bass_kernel_spmd

F32 = mybir.dt.float32
BF16 = mybir.dt.bfloat16
AF = mybir.ActivationFunctionType
ALU = mybir.AluOpType
AX = mybir.AxisListType

D = 2048; NCH = 16; FF = 5632; NF = 44; NT = 1344
TILES = [(0, 448), (448, 896), (896, 1344)]
MTILES = [(0, 384), (384, 768), (768, 1152), (1152, 1344)]


def tiles_from(c0):
    n = NT - c0
    a = ((n + 2) // 3 + 1) // 2 * 2
    return [(c0, c0 + a), (c0 + a, c0 + 2 * a), (c0 + 2 * a, NT)]
SC = 1280
SCALE = 0.125
EPS = 1e-5
POOLW = (2, 4, 8, 16)
import os
STOP_AFTER = int(os.environ.get('KSTOP', '6'))
KSMALL = int(os.environ.get('KSMALL', '0'))
KFLAGS = os.environ.get('KFLAGS', '')
MSTOP = int(os.environ.get('MSTOP', '5'))


def units(a, b):
    return range(a // 64, (b + 63) // 64)


def K(name, cs, a, b):
    return [(name, c, u) for c in cs for u in units(a, b)]


class Prog:
    def __init__(s, nc, es):
        s.nc = nc; s.es = es; s.ops = []
        s.E = {'pe': nc.tensor, 'act': nc.scalar, 'dve': nc.vector, 'pool': nc.gpsimd, 'sp': nc.sync}
        s.out_sems = set()

    def op(s, eng, fn, R=(), W=()):
        s.ops.append(('c', eng, fn, list(R), list(W), None))

    def dma(s, q, out, in_, sem, R=(), W=(), is_out=False):
        s.ops.append(('d', q, (out, in_), list(R), list(W), sem))
        if is_out:
            s.out_sems.add(sem)

    def emit(s):
        nc = s.nc; ops = s.ops; n = len(ops)
        lastw = {}; readers = {}
        deps = [None] * n
        for i, (kind, eng, fn, R, W, sem) in enumerate(ops):
            d = set()
            for k in R:
                j = lastw.get(k)
                if j is not None: d.add(j)
            for k in W:
                j = lastw.get(k)
                if j is not None: d.add(j)
                rd = readers.get(k)
                if rd:
                    d.update(rd.values())
            d.discard(i)
            deps[i] = d
            rk = eng if kind == 'c' else ('dma', i)
            for k in W:
                lastw[k] = i; readers[k] = {}
            for k in R:
                readers.setdefault(k, {})[rk] = i
        seqno = [0] * n; ectr = {}
        for i in range(n):
            if ops[i][0] == 'c':
                ectr[ops[i][1]] = ectr.get(ops[i][1], 0) + 1
                seqno[i] = ectr[ops[i][1]]
        NEAR = 10 ** 9

        def same_skip(i, j):
            if not (ops[i][0] == 'c' and ops[j][0] == 'c' and ops[i][1] == ops[j][1]):
                return False
            return ops[i][1] == 'pe' or (seqno[i] - seqno[j]) > NEAR
        needed = [False] * n
        for i in range(n):
            for j in deps[i]:
                if ops[j][0] == 'c' and not same_skip(i, j):
                    needed[j] = True
        cnt = {}; idx = [0] * n; dcount_before = [None] * n
        dsem_cnt = {}
        for i, (kind, eng, fn, R, W, sem) in enumerate(ops):
            if kind == 'c':
                if needed[i]:
                    cnt[eng] = cnt.get(eng, 0) + 1
                    idx[i] = cnt[eng]
            else:
                dsem_cnt[sem] = dsem_cnt.get(sem, 0) + 16
                idx[i] = dsem_cnt[sem]
        csem = {e: s.es.enter_context(nc.semaphore('c_' + e)) for e in ['pe', 'act', 'dve', 'pool']}
        dsem = {k: s.es.enter_context(nc.semaphore('d_' + str(k))) for k in dsem_cnt}
        seen = {e: {} for e in s.E}
        run_d = {k: 0 for k in dsem_cnt}
        nwaits = 0
        for i, (kind, eng, fn, R, W, sem) in enumerate(ops):
            Eng = s.E[eng]
            waits = {}
            for j in deps[i]:
                kj, ej = ops[j][0], ops[j][1]
                if kj == 'd':
                    key = ('d', ops[j][5]); val = run_d[ops[j][5]]
                elif same_skip(i, j):
                    continue
                else:
                    key = ('c', ej); val = idx[j]
                if val > waits.get(key, 0):
                    waits[key] = val
            for key, val in waits.items():
                if seen[eng].get(key, 0) >= val:
                    continue
                seen[eng][key] = val
                Eng.wait_ge(dsem[key[1]] if key[0] == 'd' else csem[key[1]], val)
                nwaits += 1
            if kind == 'c':
                ins = fn(Eng)
                if needed[i]:
                    ins.then_inc(csem[eng], 1)
            else:
                out, in_ = fn
                with nc.allow_non_contiguous_dma(reason="layout"):
                    Eng.dma_start(out=out, in_=in_).then_inc(dsem[sem], 16)
                run_d[sem] += 16
        for sem in sorted(s.out_sems, key=str):
            nc.sync.wait_ge(dsem[sem], dsem_cnt[sem])
        return n, nwaits


def build():
    nc = bass.Bass("TRN2", target_bir_lowering=False)
    es = ExitStack()
    P = Prog(nc, es)

    def din(name, shape):
        return nc.dram_tensor(name, list(shape), F32, kind="ExternalInput").ap()

    def dout(name, shape):
        return nc.dram_tensor(name, list(shape), F32, kind="ExternalOutput").ap()

    xin = din("xin", [NT, D])
    if KSMALL:
        Wg = Wu = Wd = [None, None]; w_in = w_out = None
        if KSMALL == 2:
            w_in = din("w_in", [2, D, 2560]); w_out = din("w_out", [2, D, D])
    else:
        Wg = [din("ffn1_gate", [2, D, FF]), din("ffn2_gate", [2, D, FF])]
        Wu = [din("ffn1_up", [2, D, FF]), din("ffn2_up", [2, D, FF])]
        Wd = [din("ffn1_down", [2, FF, D]), din("ffn2_down", [2, FF, D])]
        w_in = din("w_in", [2, D, 2560]); w_out = din("w_out", [2, D, D])
    ck = din("ck", [2, 16, 128, 256]); cv = din("cv", [2, 16, 128, 256]); spool = din("spool", [2, 16, 15, 1024])
    norms = din("norms", [7, D])
    pool_w = din("pool_w", [2, 4, 256, 256]); pscale_d = din("pool_scale", [2, 1024]); sinks_d = din("attn_sinks", [2, 16])
    identf_d = din("identf", [128, 128]); masks_d = din("masks", [128, 512]); smask_d = din("smask", [128, 1152])
    sel_d = din("sel", [128, 128]); invc_d = din("invc", [128, 64])
    y = dout("y", [1088, D]); nkp = dout("nkp", [2, 128, 256]); nvp = dout("nvp", [2, 128, 256]); npp = dout("npp", [2, 15, 1024])
    nks = dout("nks", [2, 16, 128, 256]); nvs = dout("nvs", [2, 16, 128, 256]); nps = dout("nps", [2, 16, 15, 1024])

    def sb(name, shape, dt):
        return es.enter_context(nc.sbuf_tensor(name, list(shape), dt))

    R1 = sb("R1", [128, NCH * NT], F32)
    R2 = sb("R2", [128, 21504], BF16)
    R3 = sb("R3", [128, 21504], BF16)
    R4 = sb("R4", [128, 8192], BF16)
    identf = sb("identf_s", [128, 128], F32); identb = sb("identb", [128, 128], BF16); onesb = sb("onesb", [128, 128], BF16); nhl = sb("nhl", [128, 896], BF16)
    gam = sb("gam", [128, 7 * 16], F32); pscale = sb("pscale", [128, 16], F32)
    nsink = sb("nsink", [128, 32], F32); nsrow = sb("nsrow", [128, 8], F32)
    masks = sb("masks_s", [128, 512], BF16); smask = sb("smask_s", [128, 1152], BF16); sel = sb("sel_s", [128, 128], BF16)
    invc = sb("invc_s", [128, 64], F32); epsb = sb("epsb", [128, 1], F32)
    ntmp = sb("ntmp", [128, 5 * 448], F32)
    stmp = sb("stmp", [128, 2 * 448], F32)
    PSA = es.enter_context(nc.psum_tensor("PSA", [128, 2048], F32))
    PSB = es.enter_context(nc.psum_tensor("PSB", [128, 2048], F32))

    def bank(b, n=512, off=0):
        t = PSA if b < 4 else PSB
        return t[:, (b % 4) * 512 + off:(b % 4) * 512 + off + n]

    def bankb(b):
        return bank(b).bitcast(BF16)

    x3 = R1[:, :].rearrange("p (c n) -> p c n", c=NCH)

    def r2(off, nbytes, dt=BF16):
        v = R2[:, off // 2:(off + nbytes) // 2]
        return v.bitcast(F32) if dt == F32 else v

    def r3(off, nbytes, dt=BF16):
        v = R3[:, off // 2:(off + nbytes) // 2]
        return v.bitcast(F32) if dt == F32 else v

    xn3 = R2[:, :].rearrange("p (c n) -> p c n", c=NCH)
    mix3 = R3[:, :].rearrange("p (c n) -> p c n", c=NCH)
    Hb = [r3(i * 10752, 10752).rearrange("p (f n) -> p f n", f=4) for i in range(2)]
    Wdb = [r3(21504 + i * 8192, 8192).rearrange("p (f n) -> p f n", f=4) for i in range(2)]
    stage = [r3(i * 8192, 8192, F32) for i in range(2)]
    wslot = [R4[:, i * 2048:(i + 1) * 2048].rearrange("p (k n) -> p k n", k=16) for i in range(4)]
    xnt3 = r2(0, 14336).rearrange("p (c n) -> p c n", c=NCH)
    kT3 = r2(14336, 10752).rearrange("p (g n) -> p g n", g=4)
    vtok = r2(25088, 5632).rearrange("p (b n) -> p b n", b=11)
    TM = 30720

    nsq = [ntmp[:, i * 448:(i + 1) * 448] for i in range(2)]
    nacc = ntmp[:, 896:1344]; nrs = ntmp[:, 1344:1792]; nrstd = ntmp[:, 1792:2240]
    sil = [stmp[:, i * 448:(i + 1) * 448] for i in range(2)]

    wctr = [0]

    def next_wslot():
        i = wctr[0] % 4; wctr[0] += 1
        return i

    P.dma('sp', identf[:, :], identf_d[:, :], 'cst', W=['identf'])
    P.dma('pool', masks[:, :], masks_d[:, :], 'cstp', W=['masks'])
    P.dma('pool', smask[:, :], smask_d[:, :], 'cstp', W=['smask'])
    P.dma('pool', sel[:, :], sel_d[:, :], 'cstp', W=['sel'])
    P.dma('pool', identb[:, :], identf_d[:, :], 'cstp', W=['identb'])
    P.dma('sp', invc[:, :], invc_d[:, :], 'cst', W=['invc'])
    P.dma('sp', stage[0][0:112, 0:128], norms.rearrange("j (c p) -> (j c) p", p=128), 'cst', W=[('stage', 0)])
    P.dma('sp', stage[0][0:16, 128:256], pscale_d.rearrange("l (c p) -> (l c) p", p=128), 'cst', W=[('stage', 0)])
    if 'nosink' not in KFLAGS:
        P.dma('sp', nsink[:, :], sinks_d.rearrange("l h -> (l h)").partition_broadcast(128), 'cst', W=['nsink'])
    sflat = sinks_d.rearrange("l h -> (l h)")
    for hp in range(2):
        for hh in range(2):
            src = sflat.rearrange("(q h) -> h q", h=4)[2 * hh + hp].partition_broadcast(32)
            if 'nosink' not in KFLAGS:
                P.dma('sp', nsrow[hp * 64 + hh * 32:hp * 64 + hh * 32 + 32, :], src, 'cst', W=['nsrow'])
    dummy = sb("dummy_bar", [128, 8], F32)

    def barrier(keys, W):
        P.op('dve', lambda e: e.memset(dummy[:, 0:1], 0.0), R=[], W=list(keys) + list(W))

    ALLXN = K('xn', range(NCH), 0, NT)
    ALLMIX = K('mix', range(NCH), 0, NT)
    P.op('dve', lambda e: e.memset(onesb[:, :], 1.0), W=['onesb'])
    P.op('pe', lambda e: e.transpose(bank(0)[:, 0:112], stage[0][0:112, 0:128], identf[0:112, 0:112]), R=[('stage', 0), 'identf'], W=[('ps', 0)])
    P.op('pe', lambda e: e.transpose(bank(0)[:, 128:144], stage[0][0:16, 128:256], identf[0:16, 0:16]), R=[('stage', 0), 'identf'], W=[('ps', 0)])
    P.op('dve', lambda e: e.tensor_copy(out=gam[:, :], in_=bank(0)[:, 0:112]), R=[('ps', 0)], W=['gam'])
    P.op('dve', lambda e: e.tensor_copy(out=pscale[:, :], in_=bank(0)[:, 128:144]), R=[('ps', 0)], W=['pscale'])
    P.op('dve', lambda e: e.memset(epsb[:, :], EPS), W=['epsb'])
    P.op('dve', lambda e: e.tensor_scalar(out=nsink[:, :], in0=nsink[:, :], scalar1=-1.0, scalar2=None, op0=ALU.mult), R=['nsink'], W=['nsink'])
    P.op('dve', lambda e: e.tensor_scalar(out=nsrow[:, :], in0=nsrow[:, :], scalar1=-1.0, scalar2=None, op0=ALU.mult), R=['nsrow'], W=['nsrow'])

    evq = [0]

    def evac(out, in_, R, W):
        evq[0] += 1
        if evq[0] % 2:
            P.op('act', lambda e: e.activation(out=out, in_=in_, func=AF.Copy), R=R, W=W)
        else:
            P.op('dve', lambda e: e.tensor_copy(out=out, in_=in_), R=R, W=W)

    pb = 0
    for tb in range(11):
        rows = 128 if tb < 10 else 64
        st = stage[tb % 2]
        P.dma('sp', st[0:rows, :], xin[tb * 128:tb * 128 + rows, :], 'stg%d' % (tb % 2), W=[('stage', tb % 2)])
        for cg in range(4):
            b = pb % 4; pb += 1
            for i in range(4):
                c = cg * 4 + i
                P.op('pe', lambda e, b=b, i=i, c=c, st=st, rows=rows: e.transpose(bank(b)[:, i * 128:i * 128 + rows], st[0:rows, c * 128:(c + 1) * 128], identf[0:rows, 0:rows]),
                     R=[('stage', tb % 2), 'identf'], W=[('ps', b)])
            evac(x3[:, cg * 4:cg * 4 + 4, tb * 128:tb * 128 + rows], bank(b).rearrange("p (i n) -> p i n", i=4)[:, :, 0:rows],
                 R=[('ps', b)], W=K('x', range(cg * 4, cg * 4 + 4), tb * 128, tb * 128 + rows))

    barrier([('stage', 0), ('stage', 1)], ['r3go'])

    def rmsnorm(gi, a, b, out3, okey, ocol0):
        n = b - a
        for c in range(NCH):
            sq = nsq[c % 2]
            P.op('act', lambda e, c=c, sq=sq: e.activation(out=sq[:, 0:n], in_=x3[:, c, a:b], func=AF.Square),
                 R=K('x', [c], a, b), W=[('nsq', c % 2)])
            if c == 0:
                P.op('dve', lambda e, sq=sq: e.tensor_copy(out=nacc[:, 0:n], in_=sq[:, 0:n]), R=[('nsq', 0)], W=['nacc'])
            else:
                P.op('dve', lambda e, sq=sq: e.tensor_tensor(out=nacc[:, 0:n], in0=nacc[:, 0:n], in1=sq[:, 0:n], op=ALU.add),
                     R=[('nsq', c % 2), 'nacc'], W=['nacc'])
        hi = nhl[:, 0:n]; lo = nhl[:, 448:448 + n]
        P.op('dve', lambda e: e.tensor_copy(out=hi, in_=nacc[:, 0:n]), R=['nacc'], W=['nhi'])
        P.op('dve', lambda e: e.tensor_tensor(out=lo, in0=nacc[:, 0:n], in1=hi, op=ALU.subtract), R=['nacc', 'nhi'], W=['nlo'])
        P.op('pe', lambda e: e.matmul(bank(6)[:, 0:n], lhsT=onesb[:, :], rhs=hi, start=True, stop=False), R=['onesb', 'nhi'], W=[('ps', 6)])
        P.op('pe', lambda e: e.matmul(bank(6)[:, 0:n], lhsT=onesb[:, :], rhs=lo, start=False, stop=True), R=['onesb', 'nlo'], W=[('ps', 6)])
        P.op('act', lambda e: e.activation(out=nrs[:, 0:n], in_=bank(6)[:, 0:n], func=AF.Sqrt, bias=epsb[:, 0:1], scale=1.0 / D),
             R=[('ps', 6), 'epsb'], W=['nrs'])
        P.op('dve', lambda e: e.reciprocal(out=nrstd[:, 0:n], in_=nrs[:, 0:n]), R=['nrs'], W=['nrstd'])
        for c in range(NCH):
            P.op('dve', lambda e, c=c: e.scalar_tensor_tensor(out=out3[:, c, a - ocol0:b - ocol0], in0=x3[:, c, a:b], scalar=gam[:, gi * 16 + c:gi * 16 + c + 1],
                                                               in1=nrstd[:, 0:n], op0=ALU.mult, op1=ALU.mult),
                 R=K('x', [c], a, b) + ['nrstd', 'gam'], W=K(okey, [c], a, b))

    def ffn(fi_, l, gi, c0):
        TILES = tiles_from(c0)
        wg, wu, wd = Wg[fi_][l], Wu[fi_][l], Wd[fi_][l]
        wgv = wg.rearrange("(k p) n -> p k n", p=128); wuv = wu.rearrange("(k p) n -> p k n", p=128)
        for (a, b) in TILES:
            rmsnorm(gi, a, b, xn3, 'xn', 0)
        dcnt = [0]

        def GU(gr):
            for fi in range(4):
                f = gr * 4 + fi
                sg = next_wslot(); su = next_wslot()
                P.dma('pool', wslot[sg], wgv[:, :, f * 128:(f + 1) * 128], 'w%d' % sg, W=[('w', sg)])
                P.dma('pool', wslot[su], wuv[:, :, f * 128:(f + 1) * 128], 'w%d' % su, W=[('w', su)])
                if gr >= 1:
                    DN_dma(gr - 1, fi)
                for t, (a, b) in enumerate(TILES):
                    n = b - a
                    for (s_, bk) in ((sg, 2 * t), (su, 2 * t + 1)):
                        for k in range(NCH):
                            P.op('pe', lambda e, s_=s_, bk=bk, k=k, a=a, b=b, n=n: e.matmul(bank(bk)[:, 0:n], lhsT=wslot[s_][:, k, :], rhs=xn3[:, k, a:b], start=(k == 0), stop=(k == 15)),
                                 R=[('w', s_)] + K('xn', [k], a, b), W=[('ps', bk)])
                    sl = sil[t % 2]
                    P.op('act', lambda e, t=t, n=n, sl=sl: e.activation(out=sl[:, 0:n], in_=bank(2 * t)[:, 0:n], func=AF.Silu),
                         R=[('ps', 2 * t)], W=[('sil', t % 2)])
                    P.op('dve', lambda e, t=t, n=n, sl=sl, fi=fi, a=a, b=b, gr=gr: e.tensor_tensor(out=Hb[gr % 2][:, fi, a:b], in0=sl[:, 0:n], in1=bank(2 * t + 1)[:, 0:n], op=ALU.mult),
                         R=[('sil', t % 2), ('ps', 2 * t + 1)], W=K(('H', gr % 2), [fi], a, b))

        def DN_dma(gr, q):
            wdv = wd[gr * 512:(gr + 1) * 512, :].rearrange("(f p) n -> p f n", p=128)
            hm, hq = q // 2, q % 2
            P.dma('pool', Wdb[hm][:, :, hq * 512:(hq + 1) * 512], wdv[:, :, q * 512:(q + 1) * 512], 'wd%d' % q, R=['r3go'], W=[('wd', q)])

        def DN(gr):
            for m in range(NCH):
                hm = m // 8
                for t, (a, b) in enumerate(TILES):
                    n = b - a
                    bk = (6, 7, 4, 5)[dcnt[0] % 4]; dcnt[0] += 1
                    for fi in range(4):
                        P.op('pe', lambda e, bk=bk, fi=fi, m=m, hm=hm, a=a, b=b, n=n, gr=gr: e.matmul(bank(bk)[:, 0:n], lhsT=Wdb[hm][:, fi, (m % 8) * 128:(m % 8 + 1) * 128], rhs=Hb[gr % 2][:, fi, a:b], start=(fi == 0), stop=(fi == 3)),
                             R=[('wd', m // 4)] + K(('H', gr % 2), [fi], a, b), W=[('ps', bk)])
                    P.op('dve', lambda e, bk=bk, m=m, a=a, b=b, n=n: e.scalar_tensor_tensor(out=x3[:, m, a:b], in0=bank(bk)[:, 0:n], scalar=0.5, in1=x3[:, m, a:b], op0=ALU.mult, op1=ALU.add),
                         R=[('ps', bk)] + K('x', [m], a, b), W=K('x', [m], a, b))

        NG = NF // 4
        GU(0)
        for gr in range(1, NG):
            GU(gr)
            DN(gr - 1)
        for q in range(4):
            DN_dma(NG - 1, q)
        DN(NG - 1)

    def mixer(l):
        gi = 3 * l + 1
        winv = w_in[l].rearrange("(k p) n -> p k n", p=128)
        otmp = [r2(TM + i * 1024, 1024, F32) for i in range(2)]
        oc = [0]
        pbk = [0]

        def out_tok(psrc_fn, ncols, dst_fn):
            i = oc[0] % 2; oc[0] += 1
            ot = otmp[i]
            return i, ot

        barrier(ALLXN, ['mixgo'])
        MT = TILES if l == 0 else [(128, 512), (512, 896), (896, 1344)]
        for t, (a, b) in enumerate(MT):
            n = b - a
            rmsnorm(gi, a, b, xnt3, 'xnt', a)
            for cb in range(16):
                s_ = next_wslot()
                P.dma('pool', wslot[s_], winv[:, :, cb * 128:(cb + 1) * 128], 'w%d' % s_, W=[('w', s_)])
                bk = pbk[0] % 6; pbk[0] += 1
                for k in range(NCH):
                    P.op('pe', lambda e, s_=s_, bk=bk, k=k, n=n: e.matmul(bank(bk)[:, 0:n], lhsT=wslot[s_][:, k, :], rhs=xnt3[:, k, 0:n], start=(k == 0), stop=(k == 15)),
                         R=[('w', s_)] + K('xnt', [k], a, b), W=[('ps', bk)])
                evac(mix3[:, cb, a:b], bank(bk)[:, 0:n], R=[('ps', bk)], W=K('mix', [cb], a, b))
                if b == NT and cb < 8 and 'noutok' not in KFLAGS:
                    for (ca, cbb, rows, kind) in ((1152, 1280, 128, 'p'), (1280, 1344, 64, 's')):
                        bk2 = 6 + oc[0] % 2; i = oc[0] % 2; oc[0] += 1
                        for k in range(NCH):
                            P.op('pe', lambda e, s_=s_, bk2=bk2, k=k, ca=ca, cbb=cbb, rows=rows, a=a: e.matmul(bank(bk2)[0:rows, 0:128], lhsT=xnt3[:, k, ca - a:cbb - a], rhs=wslot[s_][:, k, :], start=(k == 0), stop=(k == 15)),
                                 R=[('w', s_)] + K('xnt', [k], ca, cbb), W=[('ps', bk2)])
                        ot = otmp[i]
                        P.op('dve', lambda e, bk2=bk2, rows=rows, ot=ot: e.tensor_copy(out=ot[0:rows, 0:128], in_=bank(bk2)[0:rows, 0:128]), R=[('ps', bk2)], W=[('otmp', i)])
                        if kind == 'p':
                            P.dma('sp', npp[l, :, cb * 128:(cb + 1) * 128], ot[113:128, 0:128], 'o%d' % i, R=[('otmp', i)], is_out=True)
                        else:
                            P.dma('sp', nps[l, :, 11:15, cb * 128:(cb + 1) * 128], ot[0:64, 0:128], 'o%d' % i, R=[('otmp', i)], is_out=True)
            kslots = []
            for g in range(4):
                s_ = next_wslot(); kslots.append(s_)
                P.dma('pool', wslot[s_][:, :, 0:64], winv[:, :, 2048 + g * 64:2048 + (g + 1) * 64], 'w%d' % s_, W=[('w', s_)])
                P.op('act', lambda e, s_=s_: e.activation(out=wslot[s_][:, :, 64:128], in_=wslot[s_][:, :, 0:64], func=AF.Copy), R=[('w', s_)], W=[('w', s_)])
                bk = pbk[0] % 6; pbk[0] += 1
                for k in range(NCH):
                    P.op('pe', lambda e, s_=s_, bk=bk, k=k, n=n: e.matmul(bank(bk)[:, 0:n], lhsT=wslot[s_][:, k, :], rhs=xnt3[:, k, 0:n], start=(k == 0), stop=(k == 15)),
                         R=[('w', s_)] + K('xnt', [k], a, b), W=[('ps', bk)])
                evac(kT3[:, g, a:b], bank(bk)[:, 0:n], R=[('ps', bk)], W=K('kT', [g], a, b))
            if b == NT and 'noktok' not in KFLAGS:
                for (ca, cbb, rows, kind) in ((1152, 1280, 128, 'p'), (1280, 1344, 64, 's')):
                    bk2 = 6 + oc[0] % 2; i = oc[0] % 2; oc[0] += 1
                    for g in range(4):
                        for k in range(NCH):
                            P.op('pe', lambda e, g=g, bk2=bk2, k=k, ca=ca, cbb=cbb, rows=rows, a=a, kslots=kslots: e.matmul(bank(bk2)[0:rows, g * 64:(g + 1) * 64], lhsT=xnt3[:, k, ca - a:cbb - a], rhs=wslot[kslots[g]][:, k, 0:64], start=(k == 0), stop=(k == 15)),
                                 R=[('w', kslots[g])] + K('xnt', [k], ca, cbb), W=[('ps', bk2)])
                    ot = otmp[i]
                    P.op('dve', lambda e, bk2=bk2, rows=rows, ot=ot: e.tensor_copy(out=ot[0:rows, :], in_=bank(bk2)[0:rows, 0:256]), R=[('ps', bk2)], W=[('otmp', i)])
                    if kind == 'p':
                        P.dma('sp', nkp[l], ot[:, :], 'o%d' % i, R=[('otmp', i)], is_out=True)
                    else:
                        P.dma('sp', nks[l, :, 124:128, :], ot[0:64, :], 'o%d' % i, R=[('otmp', i)], is_out=True)
            sv = [next_wslot(), next_wslot()]
            for j in range(2):
                P.dma('pool', wslot[sv[j]], winv[:, :, 2304 + j * 128:2304 + (j + 1) * 128], 'w%d' % sv[j], W=[('w', sv[j])])
            c0 = a if 'nov' not in KFLAGS else b
            while c0 < b:
                blk = c0 // 128
                c1 = min(b, (blk + 1) * 128)
                rows = c1 - c0; po = c0 - blk * 128
                bk = pbk[0] % 6; pbk[0] += 1
                for j in range(2):
                    for k in range(NCH):
                        P.op('pe', lambda e, j=j, bk=bk, k=k, c0=c0, c1=c1, rows=rows, po=po, a=a, sv=sv: e.matmul(bank(bk)[po:po + rows, j * 128:(j + 1) * 128], lhsT=xnt3[:, k, c0 - a:c1 - a], rhs=wslot[sv[j]][:, k, :], start=(k == 0), stop=(k == 15)),
                             R=[('w', sv[j])] + K('xnt', [k], c0, c1), W=[('ps', bk)])
                if blk >= 9:
                    P.op('dve', lambda e, bk=bk, rows=rows, po=po, blk=blk: e.tensor_copy(out=vtok[po:po + rows, blk, :], in_=bank(bk)[po:po + rows, 0:256]), R=[('ps', bk)], W=K('vtok', [blk], c0, c1))
                else:
                    evac(vtok[po:po + rows, blk, :], bank(bk)[po:po + rows, 0:256], R=[('ps', bk)], W=K('vtok', [blk], c0, c1))
                if blk >= 9 and 'novout' not in KFLAGS and not (blk == 10 and 'novs' in KFLAGS) and not (blk == 9 and 'novp' in KFLAGS):
                    i = oc[0] % 2; oc[0] += 1
                    ot = otmp[i]
                    P.op('dve', lambda e, bk=bk, rows=rows, po=po, ot=ot: e.tensor_copy(out=ot[po:po + rows, :], in_=bank(bk)[po:po + rows, 0:256]), R=[('ps', bk)], W=[('otmp', i)])
                    if blk == 9:
                        P.dma('sp', nvp[l], ot[:, :], 'o%d' % i, R=[('otmp', i)], is_out=True)
                    else:
                        P.dma('sp', nvs[l, :, 124:128, :], ot[0:64, :], 'o%d' % i, R=[('otmp', i)], is_out=True)
                c0 = c1
        if 'nopass' in KFLAGS:
            return
        P.dma('sp', nks[l, :, 0:124, :], ck[l, :, 4:128, :], 'oc', is_out=True)
        P.dma('sp', nvs[l, :, 0:124, :], cv[l, :, 4:128, :], 'oc', is_out=True)
        P.dma('sp', nps[l, :, 0:11, :], spool[l, :, 4:15, :], 'oc', is_out=True)

        barrier(K('xnt', range(NCH), 0, NT), ['m2go'])
        if MSTOP < 2:
            return
        bufA = r2(0, 5120, F32); bufB = r2(5120, 5120, F32)
        hs = r2(10240, 2432, F32).rearrange("p (c s j) -> p c s j", c=2, s=16)
        dbuf = r2(TM + 2048, 5376).rearrange("p (c n) -> p c n", c=2)
        ststage = r2(TM + 2048 + 5376, 512, F32)
        pw = r2(TM + 2048 + 5376 + 512, 4096).rearrange("p (g k n) -> p g k n", g=4, k=2)
        P.dma('pool', pw, pool_w[l].rearrange("g (k p) n -> p g k n", p=128), 'pw', W=['pw'])
        for g in range(4):
            w = POOLW[g]
            for ci in range(2):
                c = 2 * g + ci
                u = mix3[:, c, 0:SC]
                src = None
                bufs = [bufA, bufB]
                cur = u; sh = 1; bi = 0
                for step in range(g + 1):
                    dst = bufs[bi]
                    P.op('pool', lambda e, dst=dst, cur=cur, sh=sh: e.tensor_tensor(out=dst[:, sh:SC], in0=cur[:, sh:SC], in1=cur[:, 0:SC - sh], op=ALU.add),
                         R=K('mix', [c], 0, SC) + [('pbuf', 1 - bi), 'm2go'], W=[('pbuf', bi)])
                    P.op('pool', lambda e, dst=dst, cur=cur, sh=sh: e.tensor_copy(out=dst[:, 0:sh], in_=cur[:, 0:sh]),
                         R=K('mix', [c], 0, SC) + [('pbuf', 1 - bi), 'm2go'], W=[('pbuf', bi)])
                    cur = dst; sh *= 2; bi = 1 - bi
                fb = 1 - bi
                P.op('dve', lambda e, cur=cur, u=u, ci=ci, w=w: e.scalar_tensor_tensor(out=dbuf[:, ci, 0:SC], in0=cur[:, 0:SC], scalar=1.0 / w, in1=u, op0=ALU.mult, op1=ALU.subtract),
                     R=[('pbuf', fb)] + K('mix', [c], 0, SC), W=K('dbuf', [ci], 0, SC))
                P.op('dve', lambda e, cur=cur, g=g: e.tensor_tensor(out=ntmp[:, 64:80], in0=cur[:, 256:272], in1=invc[:, g * 16:(g + 1) * 16], op=ALU.mult),
                     R=[('pbuf', fb), 'invc', ('nsq', 0)], W=[('nsq', 0)])
                P.op('dve', lambda e, u=u, ci=ci: e.tensor_tensor(out=dbuf[:, ci, 256:272], in0=ntmp[:, 64:80], in1=u[:, 256:272], op=ALU.subtract),
                     R=[('nsq', 0)] + K('mix', [c], 256, 272), W=K('dbuf', [ci], 256, 272))
                for bt in range(2):
                    P.dma('sp', ststage[0:120, :], spool[l, bt * 8:(bt + 1) * 8, :, c * 128:(c + 1) * 128].rearrange("b r n -> (b r) n"), 'sst', W=['ststage'])
                    P.op('pe', lambda e: e.transpose(bank(6)[:, 0:120], ststage[0:120, :], identf[0:120, 0:120]), R=['ststage', 'identf'], W=[('ps', 6)])
                    P.op('dve', lambda e, ci=ci, bt=bt: e.tensor_copy(out=hs[:, ci, bt * 8:(bt + 1) * 8, 0:15], in_=bank(6)[:, 0:120].rearrange("p (s j) -> p s j", s=8)),
                         R=[('ps', 6)], W=[('hs', ci)])
                P.op('dve', lambda e, ci=ci, c=c: e.tensor_copy(out=hs[:, ci, :, 15:19], in_=mix3[:, c, SC:NT].rearrange("p (s t) -> p s t", t=4)),
                     R=K('mix', [c], SC, NT), W=[('hs', ci)])
                for t4 in range(4):
                    P.op('dve', lambda e, ci=ci, t4=t4, w=w: e.tensor_reduce(out=bufA[:, 1280 - 64 + t4 * 16:1280 - 64 + (t4 + 1) * 16] if False else ntmp[:, t4 * 16:(t4 + 1) * 16], in_=hs[:, ci, :, 16 + t4 - w:16 + t4], op=ALU.add, axis=AX.X),
                         R=[('hs', ci), ('nsq', 0)], W=['psum4', ('nsq', 0)])
                P.op('dve', lambda e, ci=ci, c=c, w=w: e.scalar_tensor_tensor(out=dbuf[:, ci, SC:NT].rearrange("p (s t) -> p s t", t=4), in0=ntmp[:, 0:64].rearrange("p (t s) -> p s t", t=4), scalar=1.0 / w,
                                                                             in1=mix3[:, c, SC:NT].rearrange("p (s t) -> p s t", t=4), op0=ALU.mult, op1=ALU.subtract),
                     R=['psum4'] + K('mix', [c], SC, NT), W=K('dbuf', [ci], SC, NT))
            for mo in range(2):
                c = 2 * g + mo
                for t, (a, b) in enumerate(TILES):
                    n = b - a
                    bk = pbk[0] % 6; pbk[0] += 1
                    for ki in range(2):
                        P.op('pe', lambda e, bk=bk, ki=ki, mo=mo, g=g, a=a, b=b, n=n: e.matmul(bank(bk)[:, 0:n], lhsT=pw[:, g, ki, mo * 128:(mo + 1) * 128], rhs=dbuf[:, ki, a:b], start=(ki == 0), stop=(ki == 1)),
                             R=['pw'] + K('dbuf', [ki], a, b), W=[('ps', bk)])
                    P.op('act', lambda e, bk=bk, c=c, a=a, b=b, n=n: e.activation(out=mix3[:, c, a:b], in_=bank(bk)[:, 0:n], func=AF.Copy, scale=pscale[:, l * 8 + c:l * 8 + c + 1]),
                         R=[('ps', bk), 'pscale'] + K('dbuf', [0, 1], a, b), W=K('mix', [c], a, b))

        barrier([('pbuf', 0), ('pbuf', 1), ('hs', 0), ('hs', 1)], ['m3go'])
        if MSTOP < 3:
            return
        XB = 0
        pbuf = [r2(XB + i * 2048, 2048).rearrange("p (h n) -> p h n", h=4) for i in range(2)]
        ptsb = [r2(XB + 4096 + i * 2048, 2048).rearrange("p (j n) -> p j n", j=8) for i in range(2)]
        onb = [r2(XB + 8192 + i * 512, 512).rearrange("p (h n) -> p h n", h=4) for i in range(2)]
        sm = [r2(XB + 9216 + i * 128, 128, F32) for i in range(4)]
        def smv(j):
            smi = sm[j]
            return smi[:, 0:4], smi[:, 4:8], smi[:, 8:12], smi[:, 12:16], smi[:, 16:20], smi[:, 20:24]

        def stA(n_, blk, g):
            i = n_ % 2; j = n_ % 4
            Sb = (0, 1) if i == 0 else (2, 3)
            mk = masks[:, 0:256] if blk == 2 else masks[:, 256:512]
            S4 = (PSA[:, 0:1024] if i == 0 else PSA[:, 1024:2048]).rearrange("p (h n) -> p h n", h=4)
            for h in range(4):
                po = (h % 2) * 64
                reg = bank(Sb[h // 2])[:, (h % 2) * 256:(h % 2 + 1) * 256]
                P.op('pe', lambda e, reg=reg, mk=mk: e.matmul(reg, lhsT=identb[:, :], rhs=mk, start=True, stop=False),
                     R=['identb', 'masks'], W=[('ps', Sb[h // 2])])
                P.op('pe', lambda e, reg=reg, po=po, g=g, h=h, blk=blk: e.matmul(reg, lhsT=mix3[po:po + 64, 8 + 2 * g + h // 2, blk * 128:(blk + 1) * 128],
                                                                                  rhs=kT3[po:po + 64, g, (blk - 1) * 128:(blk + 1) * 128], start=False, stop=True),
                     R=K('mix', [8 + 2 * g + h // 2], blk * 128, (blk + 1) * 128) + K('kT', [g], (blk - 1) * 128, (blk + 1) * 128), W=[('ps', Sb[h // 2])])
            rmax, negm, rsum, dd, esv, rinv = smv(j)
            nsg = nsink[:, l * 16 + 4 * g:l * 16 + 4 * g + 4]
            P.op('dve', lambda e, S4=S4, rmax=rmax: e.reduce_max(out=rmax, in_=S4, axis=AX.X), R=[('ps', Sb[0]), ('ps', Sb[1])], W=[('sm', j, 0)])
            P.op('dve', lambda e, rmax=rmax, negm=negm, nsg=nsg: e.scalar_tensor_tensor(out=negm, in0=rmax, scalar=-SCALE, in1=nsg, op0=ALU.mult, op1=ALU.min), R=[('sm', j, 0), 'nsink'], W=[('sm', j, 1)])
            P.op('dve', lambda e, dd=dd, negm=negm, nsg=nsg: e.tensor_tensor(out=dd, in0=negm, in1=nsg, op=ALU.subtract), R=[('sm', j, 1), 'nsink'], W=[('sm', j, 3)])
            for h in range(4):
                P.op('act', lambda e, h=h, S4=S4, negm=negm, i=i: e.activation(out=pbuf[i][:, h, :], in_=S4[:, h, :], func=AF.Exp, bias=negm[:, h:h + 1], scale=SCALE),
                     R=[('ps', Sb[h // 2]), ('sm', j, 1)], W=[('p', i)])
            P.op('act', lambda e, dd=dd, esv=esv: e.activation(out=esv, in_=dd, func=AF.Exp), R=[('sm', j, 3)], W=[('sm', j, 4)])
            P.op('dve', lambda e, i=i, rsum=rsum: e.reduce_sum(out=rsum, in_=pbuf[i], axis=AX.X), R=[('p', i)], W=[('sm', j, 2)])
            P.op('dve', lambda e, rsum=rsum, esv=esv, rinv=rinv: e.tensor_tensor(out=rinv, in0=rsum, in1=esv, op=ALU.add), R=[('sm', j, 2), ('sm', j, 4)], W=[('sm', j, 5)])
            P.op('dve', lambda e, rinv=rinv: e.reciprocal(out=rinv, in_=rinv), R=[('sm', j, 5)], W=[('sm', j, 5)])

        def stB(n_, blk, g):
            i = n_ % 2
            ptb = 4 + i
            for h in range(4):
                for kb in range(2):
                    P.op('pe', lambda e, h=h, kb=kb, i=i, ptb=ptb: e.transpose(bankb(ptb)[:, (h * 2 + kb) * 128:(h * 2 + kb + 1) * 128], pbuf[i][:, h, kb * 128:(kb + 1) * 128], identb[:, :]),
                         R=[('p', i), 'identb'], W=[('ps', ptb)])
            P.op('act', lambda e, i=i, ptb=ptb: e.activation(out=ptsb[i], in_=bankb(ptb).rearrange("p (j n) -> p j n", j=8), func=AF.Copy), R=[('ps', ptb)], W=[('pt', i)])

        def stC(n_, blk, g):
            i = n_ % 2; j = n_ % 4
            rinv = smv(j)[5]
            ob = 6 + i
            for h in range(4):
                for kb in range(2):
                    P.op('pe', lambda e, h=h, kb=kb, i=i, ob=ob, g=g, blk=blk: e.matmul(bank(ob)[:, h * 64:(h + 1) * 64], lhsT=ptsb[i][:, h * 2 + kb, :], rhs=vtok[:, blk - 1 + kb, g * 64:(g + 1) * 64], start=(kb == 0), stop=(kb == 1)),
                         R=[('pt', i)] + K('vtok', [blk - 1 + kb], (blk - 1 + kb) * 128, (blk + kb) * 128), W=[('ps', ob)])
            P.op('dve', lambda e, i=i, ob=ob, rinv=rinv: e.tensor_tensor(out=onb[i], in0=bank(ob)[:, 0:256].rearrange("p (h n) -> p h n", h=4), in1=rinv.unsqueeze(2).broadcast_to([128, 4, 64]), op=ALU.mult),
                 R=[('ps', ob), ('sm', j, 5)], W=[('on', i)])

        def stD(n_, blk, g):
            i = n_ % 2
            ob = 6 + i
            otv = bank(ob)[:, 256:384].bitcast(BF16)
            for hh in range(2):
                P.op('pe', lambda e, hh=hh, i=i, otv=otv: e.transpose(otv[:, hh * 128:(hh + 1) * 128], onb[i][:, 2 * hh:2 * hh + 2, :].rearrange("p h n -> p (h n)"), identb[:, :]),
                     R=[('on', i), 'identb'], W=[('ps', ob)])
            P.op('act', lambda e, g=g, blk=blk, otv=otv: e.activation(out=mix3[:, 8 + 2 * g:8 + 2 * g + 2, blk * 128:(blk + 1) * 128], in_=otv.rearrange("p (h n) -> p h n", h=2), func=AF.Copy),
                 R=[('ps', ob)], W=K('mix', [8 + 2 * g, 8 + 2 * g + 1], blk * 128, (blk + 1) * 128))

        unitsl = [(blk, g) for blk in range(1 + l, 10) for g in range(4)]
        NU = len(unitsl)
        for k in range(NU + 3):
            for st, off in ((stA, 0), (stB, 1), (stC, 2), (stD, 3)):
                n_ = k - off
                if 0 <= n_ < NU:
                    st(n_, unitsl[n_][0], unitsl[n_][1])

        if MSTOP < 4:
            return
        kc = r2(XB, 2048).rearrange("p (s u d) -> p s u d", s=8, u=2)
        vc = r2(XB + 2048, 4096).rearrange("p (s n) -> p s n", s=8)
        kcT = r2(XB + 6144, 2048).rearrange("p (s n) -> p s n", s=8)
        ps_ = r2(XB + 8192, 2176)
        pts = r2(XB + 10368, 2048).rearrange("p (s n) -> p s n", s=8)
        ptn = r2(XB + 12416, 256)
        onA = r2(XB + 12672, 256); onB = r2(XB + 12928, 256)
        sms = r2(XB + 13184, 128, F32)
        qs = r2(XB + 13312, 128)
        M3K = [('on', 0), ('on', 1), ('p', 0), ('p', 1), ('pt', 0), ('pt', 1)] + [('sm', i_, j_) for i_ in range(4) for j_ in range(6)]
        barrier(M3K, ['m4go'])
        P.op('dve', lambda e: e.memset(onA[:, :], 0.0), W=['onA'])
        P.op('dve', lambda e: e.memset(onB[:, :], 0.0), W=['onB'])
        S = PSA[:, 0:1088]
        for bt in range(2):
            P.dma('pool', vc, cv[l, bt * 8:(bt + 1) * 8, :, :].rearrange("s k n -> k s n"), 'vc', R=['m4go'], W=['vc'])
            for g in range(4):
                for dup in range(2):
                    P.dma('pool', kc[:, :, dup, :], ck[l, bt * 8:(bt + 1) * 8, :, g * 64:(g + 1) * 64].rearrange("s k d -> k s d"), 'kc', R=['m4go'], W=['kc'])
                for s8 in range(8):
                    P.op('pe', lambda e, s8=s8: e.transpose(bankb(3)[:, s8 * 128:(s8 + 1) * 128], kc[:, s8, :, :].rearrange("p u d -> p (u d)"), identb[:, :]),
                         R=['kc', 'identb'], W=[('ps', 3)])
                P.op('act', lambda e: e.activation(out=kcT, in_=bankb(3).rearrange("p (s n) -> p s n", s=8), func=AF.Copy), R=[('ps', 3)], W=['kcT'])
                qcols = slice(SC + bt * 32, SC + bt * 32 + 32)
                P.op('dve', lambda e, g=g, qcols=qcols: e.tensor_copy(out=qs[:, :].rearrange("p (h n) -> p h n", h=2), in_=mix3[:, 8 + 2 * g:8 + 2 * g + 2, qcols]),
                     R=K('mix', [8 + 2 * g, 8 + 2 * g + 1], SC, NT), W=['qs'])
                for hp in range(2):
                    po = hp * 64
                    lh = qs[po:po + 64, :]
                    for s8 in range(9):
                        if s8 < 8:
                            reg = S[po:po + 64, s8 * 128:(s8 + 1) * 128]; mk = smask[:, s8 * 128:(s8 + 1) * 128]; rh = kcT[po:po + 64, s8, :]
                            RR = ['kcT', 'qs']
                        else:
                            reg = S[po:po + 64, 1024:1088]; mk = smask[:, 1024 + bt * 64:1088 + bt * 64]; rh = kT3[po:po + 64, g, SC:NT]
                            RR = K('kT', [g], SC, NT) + ['qs']
                        bkk = min(s8 // 4, 2)
                        P.op('pe', lambda e, reg=reg, mk=mk, po=po: e.matmul(reg, lhsT=identb[:, po:po + 64], rhs=mk, start=True, stop=False),
                             R=['identb', 'smask'], W=[('ps', bkk)])
                        P.op('pe', lambda e, reg=reg, lh=lh, rh=rh: e.matmul(reg, lhsT=lh, rhs=rh, start=False, stop=True),
                             R=RR, W=[('ps', bkk)])
                rmax = sms[:, 0:1]; negm = sms[:, 1:2]; rsum = sms[:, 2:3]; dd = sms[:, 3:4]; esv = sms[:, 4:5]; rinv = sms[:, 5:6]
                nsg = nsrow[:, l * 4 + g:l * 4 + g + 1]
                SK = [('ps', 0), ('ps', 1), ('ps', 2)]
                P.op('dve', lambda e, rmax=rmax: e.reduce_max(out=rmax, in_=S, axis=AX.X), R=SK, W=['sms0'])
                P.op('dve', lambda e, rmax=rmax, negm=negm: e.tensor_scalar(out=negm, in0=rmax, scalar1=-SCALE, scalar2=None, op0=ALU.mult), R=['sms0'], W=['sms1'])
                P.op('dve', lambda e, negm=negm, nsg=nsg: e.tensor_tensor(out=negm, in0=negm, in1=nsg, op=ALU.min), R=['sms1', 'nsrow'], W=['sms1'])
                P.op('act', lambda e, negm=negm: e.activation(out=ps_, in_=S, func=AF.Exp, bias=negm, scale=SCALE), R=SK + ['sms1'], W=['ps_'])
                P.op('dve', lambda e, rsum=rsum: e.reduce_sum(out=rsum, in_=ps_, axis=AX.X), R=['ps_'], W=['sms2'])
                P.op('dve', lambda e, dd=dd, negm=negm, nsg=nsg: e.tensor_tensor(out=dd, in0=negm, in1=nsg, op=ALU.subtract), R=['sms1', 'nsrow'], W=['sms3'])
                P.op('act', lambda e, dd=dd, esv=esv: e.activation(out=esv, in_=dd, func=AF.Exp), R=['sms3'], W=['sms4'])
                P.op('dve', lambda e, rsum=rsum, esv=esv, rinv=rinv: e.tensor_tensor(out=rinv, in0=rsum, in1=esv, op=ALU.add), R=['sms2', 'sms4'], W=['sms5'])
                P.op('dve', lambda e, rinv=rinv: e.reciprocal(out=rinv, in_=rinv), R=['sms5'], W=['sms5'])
                for s8 in range(8):
                    P.op('pe', lambda e, s8=s8: e.transpose(bankb(4)[:, s8 * 128:(s8 + 1) * 128], ps_[:, s8 * 128:(s8 + 1) * 128], identb[:, :]), R=['ps_', 'identb'], W=[('ps', 4)])
                P.op('pe', lambda e: e.transpose(bankb(5)[0:64, 0:128], ps_[:, 1024:1088], identb[:, :]), R=['ps_', 'identb'], W=[('ps', 5)])
                P.op('act', lambda e: e.activation(out=pts, in_=bankb(4).rearrange("p (s n) -> p s n", s=8), func=AF.Copy), R=[('ps', 4)], W=['pts'])
                P.op('dve', lambda e: e.tensor_copy(out=ptn[0:64, :], in_=bankb(5)[0:64, 0:128]), R=[('ps', 5)], W=['ptn'])
                for s8 in range(8):
                    P.op('pe', lambda e, s8=s8, g=g: e.matmul(bank(6)[:, 0:64], lhsT=pts[:, s8, :], rhs=vc[:, s8, g * 64:(g + 1) * 64], start=(s8 == 0), stop=False),
                         R=['pts', 'vc'], W=[('ps', 6)])
                P.op('pe', lambda e, g=g: e.matmul(bank(6)[:, 0:64], lhsT=ptn[0:64, :], rhs=vtok[0:64, 10, g * 64:(g + 1) * 64], start=False, stop=True),
                     R=['ptn'] + K('vtok', [10], SC, NT), W=[('ps', 6)])
                P.op('dve', lambda e, rinv=rinv: e.tensor_scalar(out=onA[:, 0:64], in0=bank(6)[:, 0:64], scalar1=rinv, scalar2=None, op0=ALU.mult), R=[('ps', 6), 'sms5'], W=['onA'])
                P.op('dve', lambda e, rinv=rinv: e.tensor_scalar(out=onB[:, 64:128], in0=bank(6)[:, 0:64], scalar1=rinv, scalar2=None, op0=ALU.mult), R=[('ps', 6), 'sms5'], W=['onB'])
                for hh in range(2):
                    P.op('pe', lambda e, hh=hh: e.matmul(bank(7)[:, hh * 32:(hh + 1) * 32], lhsT=onA[:, :], rhs=sel[:, (hh * 2 + 0) * 32:(hh * 2 + 1) * 32], start=True, stop=False),
                         R=['onA', 'sel'], W=[('ps', 7)])
                    P.op('pe', lambda e, hh=hh: e.matmul(bank(7)[:, hh * 32:(hh + 1) * 32], lhsT=onB[:, :], rhs=sel[:, (hh * 2 + 1) * 32:(hh * 2 + 2) * 32], start=False, stop=True),
                         R=['onB', 'sel'], W=[('ps', 7)])
                P.op('act', lambda e, g=g, qcols=qcols: e.activation(out=mix3[:, 8 + 2 * g:8 + 2 * g + 2, qcols], in_=bank(7)[:, 0:64].rearrange("p (h n) -> p h n", h=2), func=AF.Copy),
                     R=[('ps', 7)], W=K('mix', [8 + 2 * g, 8 + 2 * g + 1], SC, NT))

        if MSTOP < 5:
            return
        wov = w_out[l].rearrange("(k p) n -> p k n", p=128)
        dc = 0
        for mb in range(NCH):
            s_ = next_wslot()
            P.dma('pool', wslot[s_], wov[:, :, mb * 128:(mb + 1) * 128], 'w%d' % s_, W=[('w', s_)])
            for t, (a, b) in enumerate(tiles_from(128 * (l + 1))):
                n = b - a
                bk = dc % 6; dc += 1
                for k in range(NCH):
                    P.op('pe', lambda e, s_=s_, bk=bk, k=k, a=a, b=b, n=n: e.matmul(bank(bk)[:, 0:n], lhsT=wslot[s_][:, k, :], rhs=mix3[:, k, a:b], start=(k == 0), stop=(k == 15)),
                         R=[('w', s_)] + K('mix', [k], a, b), W=[('ps', bk)])
                P.op('dve', lambda e, bk=bk, mb=mb, a=a, b=b, n=n: e.tensor_tensor(out=x3[:, mb, a:b], in0=bank(bk)[:, 0:n], in1=x3[:, mb, a:b], op=ALU.add),
                     R=[('ps', bk)] + K('x', [mb], a, b), W=K('x', [mb], a, b))

    MIXK = ['qs', 'kc', 'vc', 'kcT', 'ps_', 'pts', 'ptn', 'onA', 'onB', 'pw', 'ststage', ('otmp', 0), ('otmp', 1)] + K('kT', range(4), 0, NT) + K('vtok', range(11), 0, 128) + K('dbuf', range(2), 0, NT) + K('xnt', range(NCH), 0, NT)

    ph = 0
    for l in range(2):
        for fi_, fn_ in enumerate((lambda: ffn(0, l, 3 * l + 0, 128 * l), lambda: (mixer(l), barrier(MIXK + ALLMIX, ['r3go'])), lambda: ffn(1, l, 3 * l + 2, 128 * (l + 1)))):
            if ph < STOP_AFTER and not ('noffn' in KFLAGS and fi_ != 1):
                fn_()
            ph += 1

    for (a, b) in tiles_from(256):
        if 'nonorm' not in KFLAGS:
            rmsnorm(6, a, b, x3, 'x', 0)
    pb = 0
    for tb in (range(2, 11) if 'nofinal' not in KFLAGS else []):
        rows = 128 if tb < 10 else 64
        st = stage[tb % 2]
        for cg in range(4):
            b = pb % 4; pb += 1
            for i in range(4):
                c = cg * 4 + i
                P.op('pe', lambda e, b=b, i=i, c=c, rows=rows, tb=tb: e.transpose(bank(b)[0:rows, i * 128:(i + 1) * 128], x3[:, c, tb * 128:tb * 128 + rows], identf[:, :]),
                     R=K('x', [c], tb * 128, tb * 128 + rows) + ['identf'], W=[('ps', b)])
            evac(st[0:rows, cg * 512:(cg + 1) * 512], bank(b)[0:rows, :], R=[('ps', b)], W=[('stage', tb % 2)])
        P.dma('sp', y[(tb - 2) * 128:(tb - 2) * 128 + rows, :], st[0:rows, :], 'stg%d' % (tb % 2), R=[('stage', tb % 2)], is_out=True)

    if 'dbgmix' in KFLAGS:
        dbg = dout("dbg", [128, 10752]); dbgx = dout("dbgx", [128, NCH * NT])
        P.dma('sp', dbg, R3[:, :].bitcast(F32), 'dbg', R=ALLMIX + [('stage', 0), ('stage', 1)], is_out=True)
        P.dma('sp', dbgx, R1[:, :], 'dbg', R=K('x', range(NCH), 0, NT), is_out=True)
    nops, nwaits = P.emit()
    return nc, es, nops, nwaits


_CACHE = {}


def _consts():
    identf = np.eye(128, dtype=np.float32)
    i = np.arange(128)[:, None]; j = np.arange(256)[None, :]
    rel = i + 128 - j
    band = ((rel >= 0) & (rel < 128))
    NEG = -30000.0
    maskB = np.where(band, 0.0, NEG).astype(np.float32)
    maskA0 = np.where(band & (j >= 128), 0.0, NEG).astype(np.float32)
    sm = np.full((2, 128, 1088), NEG, np.float32)
    sel = np.zeros((128, 2, 2, 32), np.float32)
    for r in range(128):
        hp = r // 64; hh = (r % 64) // 32; ii = (r % 32) // 4; t = r % 4
        sel[r, hh, hp, ii * 4 + t] = 1.0
        for bt in range(2):
            sm[bt, r, ii * 128 + t + 1:ii * 128 + 128] = 0.0
            sq = bt * 8 + ii
            sm[bt, r, 1024 + sq * 4:1024 + sq * 4 + t + 1] = 0.0
    return identf, maskA0, maskB, sm, sel.reshape(128, 128)


def kernel(x_prompt, x_sample, cache_k, cache_v, state_pool, norm_ffn1, ffn1_gate, ffn1_up, ffn1_down, norm_mix, w_in,
           pool_w, pool_scale, attn_sinks, w_out, norm_ffn2, ffn2_gate, ffn2_up, ffn2_down, final_norm):
    f32 = lambda a: np.ascontiguousarray(np.asarray(a, dtype=np.float32))
    x_prompt = f32(x_prompt); x_sample = f32(x_sample); cache_k = f32(cache_k); cache_v = f32(cache_v); state_pool = f32(state_pool)
    if 'nc' not in _CACHE:
        _CACHE['nc'] = build()
    nc = _CACHE['nc'][0]
    identf, maskA0, maskB, sm, sel = _consts()
    norms = np.stack([f32(norm_ffn1)[0], f32(norm_mix)[0], f32(norm_ffn2)[0], f32(norm_ffn1)[1], f32(norm_mix)[1], f32(norm_ffn2)[1], f32(final_norm)], 0)
    shared = dict(ffn1_gate=f32(ffn1_gate), ffn1_up=f32(ffn1_up), ffn1_down=f32(ffn1_down), ffn2_gate=f32(ffn2_gate), ffn2_up=f32(ffn2_up),
                  ffn2_down=f32(ffn2_down), w_in=f32(w_in), w_out=f32(w_out)) if not KSMALL else ({} if KSMALL == 1 else dict(w_in=f32(w_in), w_out=f32(w_out)))
    shared.update(norms=np.ascontiguousarray(norms), pool_w=f32(pool_w),
                  pool_scale=f32(pool_scale), attn_sinks=f32(attn_sinks), identf=identf, sel=sel)
    xs = x_sample.reshape(128 * 4, D)
    ckr = cache_k.reshape(2, 128, 128, 256); cvr = cache_v.reshape(2, 128, 128, 256)
    in_maps = []
    for c in range(8):
        bq, j = c // 4, c % 4
        xin = np.zeros((NT, D), np.float32)
        if j > 0:
            xin[0:256] = x_prompt[bq, j * 1024 - 256:j * 1024]
        xin[256:1280] = x_prompt[bq, j * 1024:(j + 1) * 1024]
        xin[1280:1344] = xs[c * 64:(c + 1) * 64]
        masks = np.concatenate([maskA0 if j == 0 else maskB, maskB], axis=1)
        invc = np.zeros((128, 4, 16), np.float32)
        for g, w in enumerate(POOLW):
            if j == 0:
                invc[:, g, :] = (1.0 / np.minimum(w, np.arange(16) + 1.0))[None, :]
            else:
                invc[:, g, :] = 1.0 / w
        m = dict(shared)
        m.update(xin=xin, ck=np.ascontiguousarray(ckr[:, c * 16:(c + 1) * 16]), cv=np.ascontiguousarray(cvr[:, c * 16:(c + 1) * 16]),
                 spool=np.ascontiguousarray(state_pool[:, c * 16:(c + 1) * 16]), masks=np.ascontiguousarray(masks),
                 smask=np.ascontiguousarray(np.concatenate([sm[0], sm[1][:, 1024:1088]], axis=1)), invc=np.ascontiguousarray(invc.reshape(128, 64)))
        in_maps.append(m)
    res = run_bass_kernel_spmd(nc, in_maps, core_ids=list(range(8)))
    R = res.results
    _CACHE["res"] = R
    y_prompt = np.zeros((2, 4096, D), np.float32); y_sample = np.zeros((128, 4, D), np.float32)
    nkp = np.zeros((2, 2, 128, 4, 64), np.float32); nvp = np.zeros_like(nkp); npp = np.zeros((2, 2, 15, 1024), np.float32)
    nks = np.zeros((2, 128, 128, 4, 64), np.float32); nvs = np.zeros_like(nks); nps = np.zeros((2, 128, 15, 1024), np.float32)
    for c in range(8):
        bq, j = c // 4, c % 4
        r = R[c]
        y_prompt[bq, j * 1024:(j + 1) * 1024] = r["y"][0:1024]
        y_sample[c * 16:(c + 1) * 16] = r["y"][1024:1088].reshape(16, 4, D)
        if j == 3:
            nkp[:, bq] = r["nkp"].reshape(2, 128, 4, 64); nvp[:, bq] = r["nvp"].reshape(2, 128, 4, 64); npp[:, bq] = r["npp"]
        nks[:, c * 16:(c + 1) * 16] = r["nks"].reshape(2, 16, 128, 4, 64); nvs[:, c * 16:(c + 1) * 16] = r["nvs"].reshape(2, 16, 128, 4, 64)
        nps[:, c * 16:(c + 1) * 16] = r["nps"]
    return (y_prompt, y_sample, nkp, nvp, npp, nks, nvs, nps)
```
